# Optimizing a Trainium2 kernel written in Bass

```python
import math
import jax, jax.numpy as jnp
from jax import lax
import numpy as np

D_MODEL = 1024
BATCH = 4
SEQ = 8192
DEPTH = 1
DEC_BATCH = 32
DEC_SEQ = 64
PAST_LEN = 1024

CHUNK = 64
RET_HEADS = 8
RET_DK = 64
RET_DV = 64
RET_QK = RET_HEADS * RET_DK
RET_WIDTH = RET_HEADS * RET_DV
RET_THETA = 10000.0
DIFF_HEADS = 4
DIFF_DK = 64
DIFF_DV = 2 * DIFF_DK
DIFF_QK = DIFF_HEADS * 2 * DIFF_DK
DIFF_WIDTH = DIFF_HEADS * DIFF_DV
ROPE_THETA = 500000.0
ROPE_DIM = DIFF_DK // 4
MIX_WIDTH = RET_WIDTH + DIFF_WIDTH
Q_BLOCK = 128
EPS = 1e-6
NEG_BIG = -1e30
IN_SIZES = (RET_QK, RET_QK, RET_WIDTH, RET_WIDTH, DIFF_QK, DIFF_QK, DIFF_WIDTH, DIFF_WIDTH)
IN_SPLITS = tuple(sum(IN_SIZES[:i + 1]) for i in range(len(IN_SIZES) - 1))
IN_WIDTH = sum(IN_SIZES)

kernel_name = 'hybrid_retention_diffattn_stream_step'


def rmsnorm(x, g):
    xf = x.astype(jnp.float32)
    y = xf * lax.rsqrt(jnp.mean(xf * xf, axis=-1, keepdims=True) + EPS)
    return (y * g.astype(jnp.float32)).astype(x.dtype)


def rotary_tables(pos, dim, theta):
    inv_freq = 1.0 / (theta ** (jnp.arange(0, dim, 2, dtype=jnp.float32) / dim))
    ang = pos[:, None] * inv_freq[None, :]
    return jnp.cos(ang), jnp.sin(ang)


def rotate_half(x, cos, sin):
    half = x.shape[-1] // 2
    x1, x2 = x[..., :half], x[..., half:]
    cos = cos.astype(x.dtype)
    sin = sin.astype(x.dtype)
    return jnp.concatenate([x1 * cos - x2 * sin, x2 * cos + x1 * sin], axis=-1)


def retention_log_decay():
    return jnp.log1p(-jnp.exp2(-5.0 - jnp.arange(RET_HEADS, dtype=jnp.float32)))


def branch_inputs(x, norm_g, w_in, pos):
    b, l, _ = x.shape
    h = rmsnorm(x, norm_g)
    z = jnp.einsum('bld,de->ble', h, w_in)
    rq, rk, rv, rg, dq, dk, dv, dg = jnp.split(z, IN_SPLITS, axis=-1)
    rcos, rsin = rotary_tables(pos, RET_DK, RET_THETA)
    rcos, rsin = rcos[None, :, None, :], rsin[None, :, None, :]
    rq = rotate_half(rq.reshape(b, l, RET_HEADS, RET_DK), rcos, rsin)
    rk = rotate_half(rk.reshape(b, l, RET_HEADS, RET_DK), rcos, rsin) * (RET_DK ** -0.5)
    rv = rv.reshape(b, l, RET_HEADS, RET_DV)
    dcos, dsin = rotary_tables(pos, ROPE_DIM, ROPE_THETA)
    dcos, dsin = dcos[None, :, None, None, :], dsin[None, :, None, None, :]

    def partial_rope(t):
        t = t.reshape(b, l, DIFF_HEADS, 2, DIFF_DK)
        return jnp.concatenate([rotate_half(t[..., :ROPE_DIM], dcos, dsin), t[..., ROPE_DIM:]], axis=-1)

    dq = partial_rope(dq)
    dk = partial_rope(dk)
    dv = dv.reshape(b, l, DIFF_HEADS, DIFF_DV)
    return rq, rk, rv, rg, dq, dk, dv, dg


def retention_block(q, k, v, s, log_g):
    q = q.astype(jnp.float32)
    k = k.astype(jnp.float32)
    v = v.astype(jnp.float32)
    s = s.astype(jnp.float32)
    l = q.shape[1]
    idx = jnp.arange(l, dtype=jnp.float32)
    rel = idx[:, None] - idx[None, :]
    decay = jnp.where((rel >= 0)[None], jnp.exp(log_g[:, None, None] * jnp.maximum(rel, 0.0)[None]), 0.0)
    scores = jnp.einsum('bqhd,bkhd->bhqk', q, k) * decay[None]
    inner = jnp.einsum('bhqk,bkhe->bqhe', scores, v)
    q_decay = jnp.exp(log_g[None, :] * (idx[:, None] + 1.0))
    cross = jnp.einsum('bqhd,bhde->bqhe', q, s) * q_decay[None, :, :, None]
    k_decay = jnp.exp(log_g[None, :] * (l - 1.0 - idx[:, None]))
    s_new = (jnp.exp(log_g * l)[None, :, None, None] * s
             + jnp.einsum('bkhd,bkhe->bhde', k * k_decay[None, :, :, None], v))
    return inner + cross, s_new


def retention_prompt(q, k, v, log_g):
    b, l = q.shape[0], q.shape[1]
    nc = l // CHUNK

    def to_blocks(t):
        return t.reshape(b, nc, CHUNK, RET_HEADS, t.shape[-1]).swapaxes(0, 1)

    s0 = jnp.zeros((b, RET_HEADS, RET_DK, RET_DV), jnp.float32)

    def step(s, blk):
        qc, kc, vc = blk
        o, s = retention_block(qc, kc, vc, s, log_g)
        return s, o

    s, o = lax.scan(step, s0, (to_blocks(q), to_blocks(k), to_blocks(v)))
    return o.swapaxes(0, 1).reshape(b, l, RET_HEADS, RET_DV), s


def diff_attend(q, k, v, lam, mask):
    logits = jnp.einsum('bqhcd,bkhcd->bhcqk', q, k, preferred_element_type=jnp.float32) * (DIFF_DK ** -0.5)
    if mask is not None:
        logits = jnp.where(mask, logits, NEG_BIG)
    p = jax.nn.softmax(logits, axis=-1)
    w = p[:, :, 0] - lam * p[:, :, 1]
    o = jnp.einsum('bhqk,bkhe->bqhe', w, v.astype(jnp.float32))
    return o.astype(v.dtype)


def diff_attention_prompt(q, k, v, lam):
    b, l = q.shape[0], q.shape[1]
    nb = l // Q_BLOCK
    q_blocks = q.reshape(b, nb, Q_BLOCK, DIFF_HEADS, 2, DIFF_DK).swapaxes(0, 1)
    key_chunk = jnp.arange(l) // CHUNK

    def one_block(args):
        q_blk, blk = args
        q_chunk = (blk * Q_BLOCK + jnp.arange(Q_BLOCK)) // CHUNK
        mask = key_chunk[None, :] <= q_chunk[:, None]
        return diff_attend(q_blk, k, v, lam, mask)

    o = lax.map(one_block, (q_blocks, jnp.arange(nb)))
    return o.swapaxes(0, 1).reshape(b, l, DIFF_HEADS, DIFF_DV)


def merge_branches(ret_o, ret_g, diff_o, diff_g, ret_norm_g, diff_norm_g, lambda_init, w_out):
    b, l = ret_o.shape[0], ret_o.shape[1]
    ret = rmsnorm(ret_o, ret_norm_g.reshape(RET_HEADS, RET_DV)).reshape(b, l, RET_WIDTH)
    dif = (rmsnorm(diff_o, diff_norm_g) * (1.0 - lambda_init)).reshape(b, l, DIFF_WIDTH)
    mixed = jnp.concatenate([jax.nn.silu(ret_g) * ret, jax.nn.silu(diff_g) * dif], axis=-1)
    return jnp.einsum('ble,ed->bld', mixed, w_out)


def setup_inputs(seed: int = 0) -> dict:
    key = jax.random.key(seed)
    ks = jax.random.split(key, 15)
    f32 = jnp.float32
    nrm = jax.random.normal
    return {
        'x_prompt': nrm(ks[0], (BATCH, SEQ, D_MODEL), f32),
        'x_sample': nrm(ks[1], (DEC_BATCH, DEC_SEQ, D_MODEL), f32),
        'cache_k': nrm(ks[2], (DEPTH, DEC_BATCH, PAST_LEN, DIFF_HEADS, DIFF_DV), f32),
        'cache_v': nrm(ks[3], (DEPTH, DEC_BATCH, PAST_LEN, DIFF_HEADS, DIFF_DV), f32),
        'state_ret': nrm(ks[4], (DEPTH, DEC_BATCH, RET_HEADS, RET_DK, RET_DV), f32),
        'norm_g': 1.0 + 0.02 * nrm(ks[5], (DEPTH, D_MODEL), f32),
        'w_in': nrm(ks[6], (DEPTH, D_MODEL, IN_WIDTH), f32) * (D_MODEL ** -0.5),
        'w_out': nrm(ks[7], (DEPTH, MIX_WIDTH, D_MODEL), f32) * (MIX_WIDTH ** -0.5),
        'ret_norm_g': 1.0 + 0.02 * nrm(ks[8], (DEPTH, RET_WIDTH), f32),
        'diff_norm_g': 1.0 + 0.02 * nrm(ks[9], (DEPTH, DIFF_DV), f32),
        'lam_q1': 0.1 * nrm(ks[10], (DEPTH, DIFF_DK), f32),
        'lam_k1': 0.1 * nrm(ks[11], (DEPTH, DIFF_DK), f32),
        'lam_q2': 0.1 * nrm(ks[12], (DEPTH, DIFF_DK), f32),
        'lam_k2': 0.1 * nrm(ks[13], (DEPTH, DIFF_DK), f32),
        'final_norm_g': 1.0 + 0.02 * nrm(ks[14], (D_MODEL,), f32),
    }


def reference(x_prompt, x_sample, cache_k, cache_v, state_ret, norm_g, w_in, w_out, ret_norm_g,
              diff_norm_g, lam_q1, lam_k1, lam_q2, lam_k2, final_norm_g):
    past = cache_k.shape[2]
    pos_p = jnp.arange(x_prompt.shape[1], dtype=jnp.float32)
    pos_s = past + jnp.arange(x_sample.shape[1], dtype=jnp.float32)
    log_g = retention_log_decay()
    xp, xs = x_prompt, x_sample
    bp, lp = xp.shape[0], xp.shape[1]
    bs = xs.shape[0]
    ret_p, ret_s, kp, vp, ks_new, vs_new = [], [], [], [], [], []
    for layer in range(DEPTH):
        lambda_init = 0.8 - 0.6 * math.exp(-0.3 * layer)
        lam = (jnp.exp(jnp.sum(lam_q1[layer].astype(jnp.float32) * lam_k1[layer].astype(jnp.float32)))
               - jnp.exp(jnp.sum(lam_q2[layer].astype(jnp.float32) * lam_k2[layer].astype(jnp.float32)))
               + lambda_init)
        rq, rk, rv, rg, dq, dk, dv, dg = branch_inputs(xp, norm_g[layer], w_in[layer], pos_p)
        ro, rs = retention_prompt(rq, rk, rv, log_g)
        do = diff_attention_prompt(dq, dk, dv, lam)
        xp = xp + merge_branches(ro.astype(xp.dtype), rg, do, dg, ret_norm_g[layer], diff_norm_g[layer],
                                 lambda_init, w_out[layer])
        ret_p.append(rs)
        kp.append(dk.reshape(bp, lp, DIFF_HEADS, DIFF_DV))
        vp.append(dv)
        srq, srk, srv, srg, sdq, sdk, sdv, sdg = branch_inputs(xs, norm_g[layer], w_in[layer], pos_s)
        sro, sst = retention_block(srq, srk, srv, state_ret[layer], log_g)
        k_all = jnp.concatenate(
            [cache_k[layer].reshape(bs, past, DIFF_HEADS, 2, DIFF_DK).astype(sdk.dtype), sdk], axis=1)
        v_all = jnp.concatenate([cache_v[layer].astype(sdv.dtype), sdv], axis=1)
        sdo = diff_attend(sdq, k_all, v_all, lam, None)
        xs = xs + merge_branches(sro.astype(xs.dtype), srg, sdo, sdg, ret_norm_g[layer], diff_norm_g[layer],
                                 lambda_init, w_out[layer])
        ret_s.append(sst)
        ks_new.append(sdk.reshape(bs, xs.shape[1], DIFF_HEADS, DIFF_DV))
        vs_new.append(sdv)
    y_prompt = rmsnorm(xp, final_norm_g)
    y_sample = rmsnorm(xs, final_norm_g)
    return (y_prompt, y_sample, jnp.stack(ret_p), jnp.stack(ret_s), jnp.stack(kp), jnp.stack(vp),
            jnp.stack(ks_new), jnp.stack(vs_new))
```

```python
import math
from contextlib import ExitStack

import numpy as np
import concourse.bass as bass
import concourse.mybir as mybir
from concourse.bass_utils import run_bass_kernel_spmd

F32 = mybir.dt.float32
BF16 = mybir.dt.bfloat16
AF = mybir.ActivationFunctionType
ALU = mybir.AluOpType
AX = mybir.AxisListType
EPS = 1e-6
TABW = 208


class Res:
    __slots__ = ("name", "w", "r", "acc", "excl")

    def __init__(self, name, acc=False, excl=False):
        self.name = name
        self.w = {}
        self.r = {}
        self.acc = acc
        self.excl = excl


class Eng:
    def __init__(self, nc, e, name, st):
        self.e = e
        self.name = name
        self.sem = st.enter_context(nc.semaphore("s_" + name))
        self.cnt = 0
        self.seen = {}


class DSem:
    def __init__(self, nc, name, st):
        self.name = "d_" + name
        self.sem = st.enter_context(nc.semaphore("d_" + name))
        self.cnt = 0


class Sched:
    def __init__(self, nc, st):
        self.nc = nc
        self.st = st
        self.pe = Eng(nc, nc.tensor, "pe", st)
        self.dve = Eng(nc, nc.vector, "dve", st)
        self.act = Eng(nc, nc.scalar, "act", st)
        self.pool = Eng(nc, nc.gpsimd, "pool", st)
        self.sp = Eng(nc, nc.sync, "sp", st)
        self.engs = [self.pe, self.dve, self.act, self.pool, self.sp]
        self.dsems = []

    def dsem(self, name):
        d = DSem(self.nc, name, self.st)
        self.dsems.append(d)
        return d

    def _deps(self, eng, reads, writes):
        deps = {}

        def add(d):
            for key, (sem, val) in d.items():
                if key == "pe" and eng.name == "pe":
                    continue
                if key not in deps or deps[key][1] < val:
                    deps[key] = (sem, val)

        for r in reads:
            add(r.w)
            if r.excl:
                add({k: v for k, v in r.r.items() if k != eng.name})
        for w in writes:
            add(w.w)
            add(w.r)
        for key, (sem, val) in deps.items():
            if eng.seen.get(key, 0) >= val:
                continue
            eng.e.wait_ge(sem, val)
            eng.seen[key] = val

    def _mark(self, tick, reads, writes):
        key, sem, val = tick
        for r in reads:
            if key not in r.r or r.r[key][1] < val:
                r.r[key] = (sem, val)
        for w in writes:
            if w.acc:
                w.w[key] = (sem, val)
            else:
                w.w = {key: (sem, val)}
            w.r = {}

    def op(self, eng, fn, reads=(), writes=(), signal=True):
        self._deps(eng, reads, writes)
        ins = fn(eng.e)
        tick = (eng.name, eng.sem, eng.cnt + 1)
        if signal:
            ins.then_inc(eng.sem, 1)
            eng.cnt += 1
        else:
            assert eng.name == "pe"
        self._mark(tick, reads, writes)

    def dma(self, q, dsem, out, in_, reads=(), writes=()):
        self._deps(q, reads, writes)
        q.e.dma_start(out=out, in_=in_).then_inc(dsem.sem, 16)
        dsem.cnt += 16
        self._mark((dsem.name, dsem.sem, dsem.cnt), reads, writes)

    def barrier(self):
        for e in self.engs:
            for o in self.engs:
                if o is e or o.cnt == 0:
                    continue
                if e.seen.get(o.name, 0) < o.cnt:
                    e.e.wait_ge(o.sem, o.cnt)
                    e.seen[o.name] = o.cnt
            for d in self.dsems:
                if d.cnt and e.seen.get(d.name, 0) < d.cnt:
                    e.e.wait_ge(d.sem, d.cnt)
                    e.seen[d.name] = d.cnt


class Ring:
    def __init__(self, items):
        self.items = items
        self.i = 0

    def next(self):
        it = self.items[self.i % len(self.items)]
        self.i += 1
        return it


class _Stop(Exception):
    pass


def build(NT, stop=None):
    nc = bass.Bass("TRN2", target_bir_lowering=False)
    try:
        _emit(nc, NT, stop)
    except _Stop:
        pass
    return nc


def _emit(nc, NT, stop):
    NW = 4 * NT
    NO = NW + 1
    NKS = NW + NO
    NQ = NW + 2
    def din(name, shape):
        return nc.dram_tensor(name, shape, F32, kind="ExternalInput").ap()

    def dout(name, shape):
        return nc.dram_tensor(name, shape, F32, kind="ExternalOutput").ap()

    def dscr(name, shape):
        return nc.dram_tensor(name, shape, BF16, kind="Internal").ap()

    x_oth = din("x_oth", [NO, 128, 1024])
    x_own = din("x_own", [NW, 128, 1024])
    x_smp = din("x_smp", [2, 128, 1024])
    t_oth = din("t_oth", [NO, 128, TABW])
    t_own = din("t_own", [NW, 128, TABW])
    t_smp = din("t_smp", [2, 128, TABW])
    w_in = din("w_in", [1024, 4096])
    w_out = din("w_out", [1024, 1024])
    gn_d = din("gn", [128, 1024])
    gf_d = din("gf", [128, 1024])
    gr_d = din("gr", [128, 512])
    gd_d = din("gd", [128, 1])
    lam_d = din("lamv", [128, 256])
    id_d = din("ident", [128, 128])
    dmp_d = din("dm_p", [128, 1024])
    dms_d = din("dm_s", [128, 1024])
    dcp_d = din("dect_p", [128, 20])
    dcs_d = din("dect_s", [128, 20])
    val_d = din("valid0", [128, 128])
    ckT_d = din("cache_kT", [4, 4, 128, 1024])
    cv_d = din("cache_v", [4, 1024, 512])
    sin_d = din("state_in", [4, 8, 64, 64])

    y_own = dout("y_own", [NW, 128, 1024])
    y_smp = dout("y_smp", [2, 128, 1024])
    k_own = dout("k_own", [NW, 128, 512])
    v_own = dout("v_own", [NW, 128, 512])
    k_smp = dout("k_smp", [2, 128, 512])
    v_smp = dout("v_smp", [2, 128, 512])
    st_p = dout("st_p", [8, 64, 64])
    st_s = dout("st_s", [4, 8, 64, 64])

    KT = dscr("KT", [4, 128, NKS * 128])
    VS = dscr("VS", [4, 128, NKS, 128])
    KTS = dscr("KTS", [4, 128, 256])
    VSS = dscr("VSS", [2, 128, 512])
    QT = dscr("QT", [4, 128, NQ * 128])
    GT = dscr("GT", [4, 128, NQ * 128])
    MRS = dscr("MRS", [4, 128, NQ * 128])

    with ExitStack() as st:
        S = Sched(nc, st)

        def ckpt(name):
            if stop == name:
                S.barrier()
                raise _Stop()

        def dv(fn, R, W):
            S.op(S.dve, fn, R, W)

        def ac(fn, R, W):
            S.op(S.act, fn, R, W)

        def po(fn, R, W):
            S.op(S.pool, fn, R, W)

        def pe(fn, R, W, sig=True):
            S.op(S.pe, fn, R, W, signal=sig)

        def mk_alloc(stack):
            def sb(name, shape, dt=F32):
                return stack.enter_context(nc.sbuf_tensor("sb_" + name, shape, dt))

            def ps(name, shape, dt=F32):
                return stack.enter_context(nc.psum_tensor("ps_" + name, shape, dt))

            def ring(name, shape, dt, n, dsem=False, psum=False):
                items = []
                for i in range(n):
                    t = (ps if psum else sb)(f"{name}{i}", shape, dt)
                    it = (t, Res(f"{name}{i}", excl=psum))
                    if dsem:
                        it = it + (S.dsem(f"{name}{i}"),)
                    items.append(it)
                return Ring(items)

            return sb, ps, ring

        sb0, ps0, ring0 = mk_alloc(st)

        woutb = sb0("woutb", [128, 8 * 1024], BF16)
        identb = sb0("identb", [128, 128], BF16)
        onesb = sb0("onesb", [128, 128], BF16)
        validb = sb0("validb", [128, 128], BF16)
        epsc = sb0("epsc", [128, 1])
        vcol = sb0("vcol", [128, 1])
        nlam = sb0("nlam", [128, 1])
        gd08 = sb0("gd08", [128, 1])
        Rc = Res("consts")
        Rc.acc = True
        Rwout = Res("woutb", acc=True)
        RMRS = Res("MRS", acc=True)
        RMD = [[Res(f"MD{h}_{i}") for i in range(NQ)] for h in range(4)]
        d_c = S.dsem("cst")
        d_scr = S.dsem("scr")
        d_out = S.dsem("outm")
        RKT = Res("KT", acc=True)
        RVS = Res("VS", acc=True)
        RQT = Res("QT", acc=True)
        RGT = Res("GT", acc=True)

        with ExitStack() as s1:
            sb, ps, ring = mk_alloc(s1)
            winb = sb("winb", [128, 8 * 4096], BF16)
            Rwin = Res("winb", acc=True)
            gn = sb("gn", [128, 1024])
            gr = sb("gr", [128, 512])
            dm_p = sb("dm_p", [128, 1024])
            dm_s = sb("dm_s", [128, 1024])
            dc_p = sb("dc_p", [128, 20])
            dc_s = sb("dc_s", [128, 20])
            idf = sb("idf", [128, 128])
            valf = sb("valf", [128, 128])
            lamt = sb("lamt", [128, 256])
            lamp = sb("lamp", [128, 256])
            lams = sb("lams", [128, 4])
            gdt = sb("gdt", [128, 1])
            Rtmp = Res("setup_tmp")

            for t, d in ((gn, gn_d), (gr, gr_d), (dm_p, dmp_d), (dm_s, dms_d), (dc_p, dcp_d), (dc_s, dcs_d),
                         (idf, id_d), (valf, val_d), (lamt, lam_d), (gdt, gd_d)):
                S.dma(S.sp, d_c, t[:], d[:, :], writes=[Rc])
            dv(lambda e: e.tensor_copy(identb[:], idf[:]), [Rc], [Rc])
            dv(lambda e: e.tensor_copy(validb[:], valf[:]), [Rc], [Rc])
            dv(lambda e: e.tensor_copy(vcol[:], valf[:, 0:1]), [Rc], [Rc])
            dv(lambda e: e.memset(onesb[:], 1.0), [], [Rc])
            dv(lambda e: e.memset(epsc[:], EPS), [], [Rc])
            lt4 = lamt[:].rearrange("p (a b d) -> p a b d", a=2, b=2)
            lp3 = lamp[:, 0:128].rearrange("p (a d) -> p a d", a=2)
            dv(lambda e: e.tensor_tensor(out=lp3, in0=lt4[:, :, 0, :], in1=lt4[:, :, 1, :], op=ALU.mult), [Rc], [Rtmp])
            dv(lambda e: e.tensor_reduce(out=lams[:, 0:2], in_=lp3, axis=AX.X, op=ALU.add), [Rtmp], [Rtmp])
            ac(lambda e: e.activation(out=lams[:, 2:4], in_=lams[:, 0:2], func=AF.Exp), [Rtmp], [Rtmp])
            dv(lambda e: e.tensor_tensor(out=lams[:, 0:1], in0=lams[:, 3:4], in1=lams[:, 2:3], op=ALU.subtract), [Rtmp], [Rtmp])
            lam_init = 0.8 - 0.6 * math.exp(-0.3 * 0)
            dv(lambda e: e.tensor_scalar(out=nlam[:], in0=lams[:, 0:1], scalar1=-lam_init, scalar2=None, op0=ALU.add), [Rtmp], [Rc])
            dv(lambda e: e.tensor_scalar(out=gd08[:], in0=gdt[:], scalar1=1.0 - lam_init, scalar2=None, op0=ALU.mult), [Rc], [Rc])

            RwinG = [Res(f"wing{g}", acc=True) for g in range(8)]
            d_wg = [S.dsem(f"wg{g}") for g in range(8)]
            for g in (1, 5, 2, 6, 0, 4, 3, 7):
                S.dma(S.pool, d_wg[g], winb[:].rearrange("p (kc c) -> p kc c", kc=8)[:, :, g * 512:(g + 1) * 512],
                      w_in[:, g * 512:(g + 1) * 512].rearrange("(kc p) c -> p kc c", p=128), writes=[RwinG[g]])
            d_wo = S.dsem("wo")

            def load_wout():
                S.dma(S.pool, d_wo, woutb[:].rearrange("p (kc c) -> p kc c", kc=8), w_out.rearrange("(kc p) c -> p kc c", p=128), writes=[Rwout])

            ckpt("setup")
            xring = ring("xt", [128, 1024], F32, 2, dsem=True)
            tring = ring("tb", [128, TABW], F32, 2, dsem=True)
            junk = sb("junk", [128, 1024], BF16)
            Rjunk = Res("junk")
            ssr = ring("ss", [128, 2], F32, 2)
            hbr = ring("hb", [128, 1024], BF16, 2)
            hTr = ring("hT", [128, 1024], BF16, 2)
            f32r = ring("f32t", [128, 512], F32, 6)
            kor = ring("ko", [128, 512], F32, 2, dsem=True)
            vor = ring("vo", [128, 512], F32, 2, dsem=True)
            ktr = ring("kts", [128, 512], BF16, 2, dsem=True)
            qtr = ring("qts", [128, 512], BF16, 2, dsem=True)
            gtr = ring("gts", [128, 512], BF16, 2, dsem=True)
            vbr = ring("vbs", [128, 512], BF16, 2, dsem=True)
            qkTr = ring("qkT", [128, 1024], BF16, 2)
            qab = sb("qab", [128, 1024], BF16)
            Rqab = Res("qab")
            sTm = sb("sTm", [128, 1024], BF16)
            RsTm = Res("sTm")
            t128 = sb("t128", [128, 128])
            Rt128 = Res("t128")
            ss8 = sb("ss8", [128, 8])
            Rss8 = Res("ss8")
            Sf = sb("Sf", [128, 256])
            Sb_ = sb("Sb", [128, 256], BF16)
            RSf, RSb = Res("Sf"), Res("Sb")
            d_stp = S.dsem("stp")

            pzr = ring("pz", [128, 512], F32, 3, psum=True)
            pT, RpT = ps("pT", [128, 1024], BF16), Res("pT", excl=True)
            pT2, RpT2 = ps("pT2", [128, 1024], BF16), Res("pT2", excl=True)
            pSc, RpSc = ps("pSc", [128, 512], F32), Res("pSc", excl=True)
            pIO, RpIO = ps("pIO", [128, 512], F32), Res("pIO", excl=True)
            pSt, RpSt = ps("pSt", [128, 512], F32), Res("pSt", excl=True)

            dv(lambda e: e.memset(Sf[:], 0.0), [], [RSf])
            dv(lambda e: e.memset(Sb_[:], 0.0), [], [RSb])

            xsrc = {"oth": x_oth, "own": x_own, "smp": x_smp}
            tsrc = {"oth": t_oth, "own": t_own, "smp": t_smp}

            def inproj(g, hT, RhT):
                pz_, Rpz = pzr.next()
                for kc in range(8):
                    pe(lambda e, kc=kc: e.matmul(pz_[:], lhsT=hT[:, kc * 128:(kc + 1) * 128],
                                                 rhs=winb[:, kc * 4096 + g * 512: kc * 4096 + (g + 1) * 512],
                                                 start=(kc == 0), stop=(kc == 7)),
                       [RhT, RwinG[g]], [Rpz], sig=(kc == 7))
                return pz_, Rpz

            def rot_ret(zp, Rz, tb, Rt, outb, Rout, scale):
                t1, Rt1 = f32r.next()
                t2, Rt2 = f32r.next()
                z3 = zp[:].rearrange("p (h d) -> p h d", h=8)
                z4 = zp[:].rearrange("p (h t d) -> p h t d", h=8, t=2)
                t13 = t1[:].rearrange("p (h d) -> p h d", h=8)
                t24 = t2[:].rearrange("p (h t d) -> p h t d", h=8, t=2)
                cosb = tb[:, 0:64].unsqueeze(1).to_broadcast([128, 8, 64])
                nsb = tb[:, 64:96].unsqueeze(1).to_broadcast([128, 8, 32])
                psb = tb[:, 96:128].unsqueeze(1).to_broadcast([128, 8, 32])
                dv(lambda e: e.scalar_tensor_tensor(out=t13, in0=z3, scalar=scale, in1=cosb, op0=ALU.mult, op1=ALU.mult), [Rz, Rt], [Rt1])
                dv(lambda e: e.scalar_tensor_tensor(out=t24[:, :, 0, :], in0=z4[:, :, 1, :], scalar=scale, in1=nsb, op0=ALU.mult, op1=ALU.mult), [Rz, Rt], [Rt2])
                dv(lambda e: e.scalar_tensor_tensor(out=t24[:, :, 1, :], in0=z4[:, :, 0, :], scalar=scale, in1=psb, op0=ALU.mult, op1=ALU.mult), [Rz, Rt], [Rt2])
                dv(lambda e: e.tensor_tensor(out=outb[:], in0=t1[:], in1=t2[:], op=ALU.add), [Rt1, Rt2], [Rout])

            def rot_diff(zp, Rz, tb, Rt, outf, Rout):
                z3 = zp[:].rearrange("p (g d) -> p g d", g=8)
                o3 = outf[:].rearrange("p (g d) -> p g d", g=8)
                t3 = t128[:].rearrange("p (g d) -> p g d", g=8)
                cosb = tb[:, 128:192].unsqueeze(1).to_broadcast([128, 8, 64])
                nsb = tb[:, 192:200].unsqueeze(1).to_broadcast([128, 8, 8])
                psb = tb[:, 200:208].unsqueeze(1).to_broadcast([128, 8, 8])
                dv(lambda e: e.tensor_tensor(out=o3, in0=z3, in1=cosb, op=ALU.mult), [Rz, Rt], [Rout])
                dv(lambda e: e.tensor_tensor(out=t3[:, :, 0:8], in0=z3[:, :, 8:16], in1=nsb, op=ALU.mult), [Rz, Rt], [Rt128])
                dv(lambda e: e.tensor_tensor(out=t3[:, :, 8:16], in0=z3[:, :, 0:8], in1=psb, op=ALU.mult), [Rz, Rt], [Rt128])
                dv(lambda e: e.tensor_tensor(out=o3[:, :, 0:16], in0=o3[:, :, 0:16], in1=t3, op=ALU.add), [Rout, Rt128], [Rout])

            def transp4(src, Rsrc, pdst, Rpd, off):
                for c in range(4):
                    pe(lambda e, c=c: e.transpose(pdst[:, off + c * 128: off + (c + 1) * 128], src[:, c * 128:(c + 1) * 128], identb[:]),
                       [Rsrc, Rc], [Rpd], sig=(c == 3))

            mrr = ring("mrs", [128, 512], BF16, 2, dsem=True)
            XS = []
            for p_ in range(3):
                d_ = {}
                for nm in ("Kb", "Kd", "Vb", "Qb", "dkb", "dqb"):
                    d_[nm] = sb(f"{nm}{p_}", [128, 512], BF16)
                    d_["R" + nm] = Res(f"{nm}{p_}")
                for nm in ("gate", "ro"):
                    d_[nm] = sb(f"{nm}{p_}", [128, 512])
                    d_["R" + nm] = Res(f"{nm}{p_}")
                XS.append(d_)
            tcr = ring("tc", [128, 512], F32, 2)
            sqt, Rsqt = sb("sqt", [128, 512]), Res("sqt")
            mixr = ring("mixb", [128, 512], BF16, 2)
            SfS2 = [[sb(f"SfS{b_}_{r}", [128, 256]) for r in range(2)] for b_ in range(2)]
            SbS2 = [[sb(f"SbS{b_}_{r}", [128, 256], BF16) for r in range(2)] for b_ in range(2)]
            RSfS2 = [[Res(f"SfS{b_}_{r}") for r in range(2)] for b_ in range(2)]
            RSbS2 = [[Res(f"SbS{b_}_{r}") for r in range(2)] for b_ in range(2)]
            d_st2 = [[S.dsem(f"sti{b_}{r}") for r in range(2)] for b_ in range(2)]
            d_sts2 = [[S.dsem(f"sto{b_}{r}") for r in range(2)] for b_ in range(2)]

            def P0(c):
                kind, idx = c["kind"], c["idx"]
                smp = kind == "smp"
                xt, Rx, dx = xring.next()
                tb, Rt, dtb = tring.next()
                S.dma(S.sp, dx, xt[:], xsrc[kind][idx], writes=[Rx])
                S.dma(S.sp, dtb, tb[:], tsrc[kind][idx], writes=[Rt])
                if smp:
                    for r in range(2):
                        for par in range(2):
                            src = sin_d[2 * idx + r].rearrange("(j par) d e -> par d j e", par=2)[par]
                            S.dma(S.sp, d_st2[idx][r], SfS2[idx][r][par * 64:(par + 1) * 64, :].rearrange("p (j e) -> p j e", j=4), src,
                                  writes=[RSfS2[idx][r]])
                        ac(lambda e, r=r: e.copy(out=SbS2[idx][r][:], in_=SfS2[idx][r][:]), [RSfS2[idx][r]], [RSbS2[idx][r]])
                ss, Rss = ssr.next()
                hb, Rhb = hbr.next()
                hT, RhT = hTr.next()
                ac(lambda e: e.activation(out=junk[:], in_=xt[:], func=AF.Square, accum_out=ss[:, 0:1]), [Rx], [Rjunk, Rss])
                ac(lambda e: e.activation(out=ss[:, 1:2], in_=ss[:, 0:1], func=AF.Sqrt, bias=epsc[:, 0:1], scale=1.0 / 1024), [Rss, Rc], [Rss])
                dv(lambda e: e.reciprocal(out=ss[:, 1:2], in_=ss[:, 1:2]), [Rss], [Rss])
                dv(lambda e: e.scalar_tensor_tensor(out=hb[:], in0=xt[:], scalar=ss[:, 1:2], in1=gn[:], op0=ALU.mult, op1=ALU.mult), [Rx, Rss, Rc], [Rhb])
                c.update(tb=tb, Rt=Rt, hT=hT, RhT=RhT, hb=hb, Rhb=Rhb)

            def P0b(c):
                hb, Rhb, hT, RhT = c["hb"], c["Rhb"], c["hT"], c["RhT"]
                for kc in range(8):
                    pe(lambda e, kc=kc: e.transpose(pT[:, kc * 128:(kc + 1) * 128], hb[:, kc * 128:(kc + 1) * 128], identb[:]),
                       [Rhb, Rc], [RpT], sig=(kc == 7))
                ac(lambda e: e.copy(out=hT[:], in_=pT[:]), [RpT], [RhT])

            def P1(c):
                kind, idx, ks = c["kind"], c["idx"], c["ks"]
                own = kind != "oth"
                smp = kind == "smp"
                qb = idx if kind == "own" else NW + idx
                dc = dc_s if smp else dc_p
                X = XS[c["seq"] % 3]
                tb, Rt, hT, RhT = c["tb"], c["Rt"], c["hT"], c["RhT"]
                zk, Rzk = inproj(1, hT, RhT)
                rot_ret(zk, Rzk, tb, Rt, X["Kb"], X["RKb"], 0.125)
                po(lambda e: e.tensor_tensor(out=X["Kd"][:].rearrange("p (h d) -> p h d", h=8), in0=X["Kb"][:].rearrange("p (h d) -> p h d", h=8),
                                             in1=dc[:, 8:16].unsqueeze(2).to_broadcast([128, 8, 64]), op=ALU.mult), [X["RKb"], Rc], [X["RKd"]])
                yield
                if own:
                    zq, Rzq = inproj(0, hT, RhT)
                    rot_ret(zq, Rzq, tb, Rt, X["Qb"], X["RQb"], 1.0)
                    yield
                zdk, Rzdk = inproj(5, hT, RhT)
                ko, Rko, dko = kor.next()
                rot_diff(zdk, Rzdk, tb, Rt, ko, Rko)
                if own:
                    S.dma(S.pool, dko, (k_smp if smp else k_own)[idx], ko[:], reads=[Rko])
                ac(lambda e: e.copy(out=X["dkb"][:], in_=ko[:]), [Rko], [X["Rdkb"]])
                yield
                if own:
                    zdq, Rzdq = inproj(4, hT, RhT)
                    qf, Rqf = f32r.next()
                    rot_diff(zdq, Rzdq, tb, Rt, qf, Rqf)
                    ac(lambda e: e.copy(out=X["dqb"][:], in_=qf[:]), [Rqf], [X["Rdqb"]])
                    yield
                zv, Rzv = inproj(2, hT, RhT)
                ac(lambda e: e.copy(out=X["Vb"][:], in_=zv[:]), [Rzv], [X["RVb"]])
                yield
                zdv, Rzdv = inproj(6, hT, RhT)
                vbs, Rvbs, dvbs = vbr.next()
                ac(lambda e: e.copy(out=vbs[:], in_=zdv[:]), [Rzdv], [Rvbs])
                if smp:
                    S.dma(S.pool, dvbs, VSS[idx], vbs[:], reads=[Rvbs], writes=[RVS])
                else:
                    S.dma(S.pool, dvbs, VS[:, :, ks, :].rearrange("h p e -> p h e"), vbs[:].rearrange("p (h e) -> p h e", h=4), reads=[Rvbs], writes=[RVS])
                if own:
                    vo, Rvo, dvo = vor.next()
                    dv(lambda e: e.tensor_copy(vo[:], zdv[:]), [Rzdv], [Rvo])
                    S.dma(S.pool, dvo, (v_smp if smp else v_own)[idx], vo[:], reads=[Rvo])
                yield
                if own:
                    zg, Rzg = inproj(3, hT, RhT)
                    ac(lambda e: e.activation(out=X["gate"][:], in_=zg[:], func=AF.Silu), [Rzg], [X["Rgate"]])
                    yield
                    pzg, Rpzg = pzr.next()
                    for h in range(4):
                        for kc in range(8):
                            pe(lambda e, h=h, kc=kc: e.matmul(pzg[:, h * 128:(h + 1) * 128],
                                                              lhsT=winb[:, kc * 4096 + 3584 + h * 128: kc * 4096 + 3584 + (h + 1) * 128],
                                                              rhs=hT[:, kc * 128:(kc + 1) * 128], start=(kc == 0), stop=(kc == 7)),
                               [RwinG[7], RhT], [Rpzg], sig=(h == 3 and kc == 7))
                    gts, Rgts, dgts = gtr.next()
                    ac(lambda e: e.activation(out=gts[:], in_=pzg[:], func=AF.Silu), [Rpzg], [Rgts])
                    S.dma(S.pool, dgts, GT[:, :, qb * 128:(qb + 1) * 128].rearrange("h p t -> p h t"),
                          gts[:].rearrange("p (h t) -> p h t", h=4), reads=[Rgts], writes=[RGT])
                    yield

            def P2(c):
                kind, idx, ks = c["kind"], c["idx"], c["ks"]
                own = kind != "oth"
                smp = kind == "smp"
                qb = idx if kind == "own" else NW + idx
                dm = dm_s if smp else dm_p
                dc = dc_s if smp else dc_p
                X = XS[c["seq"] % 3]
                Kb, RKb, Kd, RKd, Vb, RVb, Qb, RQb = X["Kb"], X["RKb"], X["Kd"], X["RKd"], X["Vb"], X["RVb"], X["Qb"], X["RQb"]
                dkb, Rdkb, dqb, Rdqb, gate, Rgate, ro, Rro = X["dkb"], X["Rdkb"], X["dqb"], X["Rdqb"], X["gate"], X["Rgate"], X["ro"], X["Rro"]
                if own:
                    transp4(Qb, RQb, pT2, RpT2, 0)
                    transp4(Kb, RKb, pT2, RpT2, 512)
                    qkT, RqkT = qkTr.next()
                    ac(lambda e: e.copy(out=qkT[:], in_=pT2[:]), [RpT2], [RqkT])
                    yield
                transp4(dkb, Rdkb, pT2, RpT2, 512)
                kts, Rkts, dkts = ktr.next()
                dv(lambda e: e.tensor_copy(kts[:], pT2[:, 512:1024]), [RpT2], [Rkts])
                dstK = KTS[:, :, idx * 128:(idx + 1) * 128] if smp else KT[:, :, ks * 128:(ks + 1) * 128]
                S.dma(S.pool, dkts, dstK.rearrange("h p t -> p h t"), kts[:].rearrange("p (h t) -> p h t", h=4), reads=[Rkts], writes=[RKT])
                yield
                if own:
                    transp4(dqb, Rdqb, pT2, RpT2, 0)
                    qts, Rqts, dqts = qtr.next()
                    dv(lambda e: e.tensor_copy(qts[:], pT2[:, 0:512]), [RpT2], [Rqts])
                    S.dma(S.pool, dqts, QT[:, :, qb * 128:(qb + 1) * 128].rearrange("h p t -> p h t"),
                          qts[:].rearrange("p (h t) -> p h t", h=4), reads=[Rqts], writes=[RQT])
                    yield
                    if smp:
                        SbS, RSbS = SbS2[idx], RSbS2[idx]
                        q3 = qkT[:, 0:512].rearrange("p (c t) -> p c t", c=4)
                        a3 = qab[:, 0:512].rearrange("p (c t) -> p c t", c=4)
                        b3 = qab[:, 512:1024].rearrange("p (c t) -> p c t", c=4)
                        po(lambda e: e.memset(qab[:], 0.0), [], [Rqab])
                        po(lambda e: e.tensor_copy(a3[:, :, 0:64], q3[:, :, 0:64]), [RqkT], [Rqab])
                        po(lambda e: e.tensor_copy(b3[:, :, 64:128], q3[:, :, 64:128]), [RqkT], [Rqab])
                    ro3 = ro[:].rearrange("p (h e) -> p h e", h=8)
                    for half in range(2):
                        for hh in range(4):
                            h = 4 * half + hh
                            j, p0 = h // 2, (h % 2) * 64
                            pb, Rpb = (pSc, RpSc) if hh % 2 == 0 else (pSt, RpSt)
                            pe(lambda e, hh=hh, j=j, p0=p0, pb=pb: e.matmul(pb[:, hh * 128:(hh + 1) * 128],
                                                                            lhsT=qkT[p0:p0 + 64, 512 + j * 128: 512 + (j + 1) * 128],
                                                                            rhs=qkT[p0:p0 + 64, j * 128:(j + 1) * 128], start=True, stop=True),
                               [RqkT], [Rpb], sig=(hh >= 2))
                        for b_ in range(2):
                            pb, Rpb = (pSc, RpSc) if b_ == 0 else (pSt, RpSt)
                            v4 = lambda ap, b_=b_: ap.rearrange("p (a b t) -> p a b t", a=2, b=2)[:, :, b_, :]
                            dv(lambda e, half=half, pb=pb, v4=v4: e.tensor_tensor(out=v4(sTm[:, half * 512:(half + 1) * 512]), in0=v4(pb[:]),
                                                                                  in1=v4(dm[:, half * 512:(half + 1) * 512]), op=ALU.mult),
                               [Rpb, Rc], [RsTm])
                        yield
                        for hh in range(4):
                            h = 4 * half + hh
                            j, p0 = h // 2, (h % 2) * 64
                            pe(lambda e, hh=hh, h=h: e.matmul(pIO[:, hh * 128: hh * 128 + 64], lhsT=sTm[:, h * 128:(h + 1) * 128],
                                                              rhs=Vb[:, h * 64:(h + 1) * 64], start=True, stop=True),
                               [RsTm, RVb], [RpIO], sig=False)
                            if not smp:
                                pe(lambda e, hh=hh, j=j, p0=p0: e.matmul(pIO[:, hh * 128 + 64:(hh + 1) * 128],
                                                                         lhsT=qkT[p0:p0 + 64, j * 128:(j + 1) * 128],
                                                                         rhs=Sb_[p0:p0 + 64, j * 64:(j + 1) * 64], start=True, stop=True),
                                   [RqkT, RSb], [RpIO], sig=(hh == 3))
                            else:
                                for r in range(2):
                                    pe(lambda e, hh=hh, j=j, p0=p0, r=r: e.matmul(pIO[:, hh * 128 + 64:(hh + 1) * 128],
                                                                                  lhsT=qab[p0:p0 + 64, r * 512 + j * 128: r * 512 + (j + 1) * 128],
                                                                                  rhs=SbS[r][p0:p0 + 64, j * 64:(j + 1) * 64],
                                                                                  start=(r == 0), stop=(r == 1)),
                                       [Rqab, RSbS[r]], [RpIO], sig=(hh == 3 and r == 1))
                        pio4 = pIO[:].rearrange("p (h t e) -> p h t e", h=4, t=2)
                        tc_, Rtc = tcr.next()
                        tc3 = tc_[:, 0:256].rearrange("p (h e) -> p h e", h=4)
                        dv(lambda e, half=half: e.tensor_tensor(out=tc3, in0=pio4[:, :, 1, :],
                                                                in1=dc[:, 4 * half:4 * half + 4].unsqueeze(2).to_broadcast([128, 4, 64]),
                                                                op=ALU.mult), [RpIO, Rc], [Rtc])
                        dv(lambda e, half=half: e.tensor_tensor(out=ro3[:, 4 * half:4 * half + 4, :], in0=tc3, in1=pio4[:, :, 0, :], op=ALU.add),
                           [Rtc, RpIO], [Rro])
                        yield
                    dv(lambda e: e.tensor_tensor(out=sqt[:], in0=ro[:], in1=ro[:], op=ALU.mult), [Rro], [Rsqt])
                    dv(lambda e: e.tensor_reduce(out=ss8[:], in_=sqt[:].rearrange("p (h e) -> p h e", h=8), axis=AX.X, op=ALU.add), [Rsqt], [Rss8])
                    ac(lambda e: e.activation(out=ss8[:], in_=ss8[:], func=AF.Sqrt, bias=epsc[:, 0:1], scale=1.0 / 64), [Rss8, Rc], [Rss8])
                    dv(lambda e: e.reciprocal(out=ss8[:], in_=ss8[:]), [Rss8], [Rss8])
                    dv(lambda e: e.tensor_tensor(out=ro3, in0=ro3, in1=ss8[:].unsqueeze(2).to_broadcast([128, 8, 64]), op=ALU.mult), [Rro, Rss8], [Rro])
                    dv(lambda e: e.tensor_tensor(out=ro[:], in0=ro[:], in1=gr[:], op=ALU.mult), [Rro, Rc], [Rro])
                    mixb, Rmixb = mixr.next()
                    dv(lambda e: e.tensor_tensor(out=mixb[:], in0=ro[:], in1=gate[:], op=ALU.mult), [Rro, Rgate], [Rmixb])
                    yield
                    transp4(mixb, Rmixb, pT2, RpT2, 0)
                    mrs, Rmrs, dmrs = mrr.next()
                    ac(lambda e: e.copy(out=mrs[:], in_=pT2[:, 0:512]), [RpT2], [Rmrs])
                    S.dma(S.pool, dmrs, MRS[:, :, qb * 128:(qb + 1) * 128].rearrange("c p t -> p c t"),
                          mrs[:].rearrange("p (c t) -> p c t", c=4), reads=[Rmrs], writes=[RMRS])
                    yield
                if not smp:
                    for j in range(4):
                        pe(lambda e, j=j: e.matmul(pSt[:, j * 128:(j + 1) * 128], lhsT=Kd[:, j * 128:(j + 1) * 128],
                                                   rhs=Vb[:, j * 128:(j + 1) * 128], start=True, stop=True),
                           [RKd, RVb], [RpSt], sig=(j == 3))
                    for par in range(2):
                        prt = slice(par * 64, par * 64 + 64)
                        Sf3 = Sf[prt, :].rearrange("p (j e) -> p j e", j=4)
                        pst3 = pSt[prt, :].rearrange("p (j x) -> p j x", j=4)[:, :, par * 64:par * 64 + 64]
                        gb = dc[prt, 16:20].unsqueeze(2).to_broadcast([64, 4, 64])
                        dv(lambda e, Sf3=Sf3, gb=gb: e.tensor_tensor(out=Sf3, in0=Sf3, in1=gb, op=ALU.mult), [RSf, Rc], [RSf])
                        dv(lambda e, Sf3=Sf3, pst3=pst3: e.tensor_tensor(out=Sf3, in0=Sf3, in1=pst3, op=ALU.add), [RSf, RpSt], [RSf])
                    ac(lambda e: e.copy(out=Sb_[:], in_=Sf[:]), [RSf], [RSb])
                    if c.get("last_prompt"):
                        for par in range(2):
                            dst = st_p.rearrange("(j par) d e -> par d j e", par=2)[par]
                            S.dma(S.pool, d_stp, dst, Sf[par * 64:(par + 1) * 64, :].rearrange("p (j e) -> p j e", j=4), reads=[RSf])
                else:
                    SfS, RSfS = SfS2[idx], RSfS2[idx]
                    for r in range(2):
                        rows = slice(r * 64, r * 64 + 64)
                        pst, Rpst = (pSt, RpSt) if r == 0 else (pSc, RpSc)
                        for j in range(4):
                            pe(lambda e, j=j, pst=pst, rows=rows: e.matmul(pst[:, j * 128:(j + 1) * 128], lhsT=Kd[rows, j * 128:(j + 1) * 128],
                                                                           rhs=Vb[rows, j * 128:(j + 1) * 128], start=True, stop=True),
                               [RKd, RVb], [Rpst], sig=(j == 3))
                        for par in range(2):
                            prt = slice(par * 64, par * 64 + 64)
                            Sf3 = SfS[r][prt, :].rearrange("p (j e) -> p j e", j=4)
                            pst3 = pst[prt, :].rearrange("p (j x) -> p j x", j=4)[:, :, par * 64:par * 64 + 64]
                            gb = dc[prt, 16:20].unsqueeze(2).to_broadcast([64, 4, 64])
                            dv(lambda e, Sf3=Sf3, gb=gb: e.tensor_tensor(out=Sf3, in0=Sf3, in1=gb, op=ALU.mult), [RSfS[r], Rc], [RSfS[r]])
                            dv(lambda e, Sf3=Sf3, pst3=pst3: e.tensor_tensor(out=Sf3, in0=Sf3, in1=pst3, op=ALU.add), [RSfS[r], Rpst], [RSfS[r]])
                        for par in range(2):
                            dst = st_s[2 * idx + r].rearrange("(j par) d e -> par d j e", par=2)[par]
                            S.dma(S.pool, d_sts2[idx][r], dst, SfS[r][par * 64:(par + 1) * 64, :].rearrange("p (j e) -> p j e", j=4), reads=[RSfS[r]])
                yield

            seq = [("oth", 0, 0)]
            for i in range(NW):
                seq += [("own", i, 2 * i + 1), ("oth", i + 1, 2 * i + 2)]
            n_prompt = len(seq)
            seq += [("smp", 0, 0), ("smp", 1, 0)]
            ctxs = [dict(kind=k_, idx=i_, ks=ks_, seq=n_) for n_, (k_, i_, ks_) in enumerate(seq)]
            ctxs[n_prompt - 1]["last_prompt"] = True

            def interleave(g1, g2, hook=None, hook_at=2):
                gens = [g for g in (g1, g2) if g is not None]
                rnd = 0
                while gens:
                    for g in list(gens):
                        try:
                            next(g)
                        except StopIteration:
                            gens.remove(g)
                    rnd += 1
                    if hook is not None and rnd == hook_at:
                        hook()
                        hook = None
                if hook is not None:
                    hook()

            P0(ctxs[0])
            P0b(ctxs[0])
            for n_, c_ in enumerate(ctxs):
                nxt = ctxs[n_ + 1] if n_ + 1 < len(ctxs) else None
                if nxt is not None:
                    P0(nxt)
                interleave(P1(c_), P2(ctxs[n_ - 2]) if n_ >= 2 else None, hook=(lambda nxt=nxt: P0b(nxt)) if nxt is not None else None)
                if n_ == 1:
                    load_wout()
            for c_ in ctxs[-2:]:
                interleave(P2(c_), None)
            S.barrier()
            ckpt("p1")

        with ExitStack() as s23:
            sb, ps, ring = mk_alloc(s23)
            MD = sb("MD", [128, 4 * NQ * 128], BF16)
            MD3 = MD[:].rearrange("p (c t) -> p c t", c=4)
            gf = sb("gf", [128, 1024])
            Rgf = Res("gf")
            xr = ring("x3", [128, 1024], F32, 3, dsem=True)
            mr3r = ring("mr3", [128, 512], BF16, 3, dsem=True)
            p3 = {}

            def p3_load(qb):
                src = x_smp[qb - NW] if qb >= NW else x_own[qb]
                xt, Rx, dx = xr.next()
                S.dma(S.sp, dx, xt[:], src, writes=[Rx])
                mr, Rmr, dmr = mr3r.next()
                S.dma(S.sp, dmr, mr[:].rearrange("p (c t) -> p c t", c=4), MRS[:, :, qb * 128:(qb + 1) * 128].rearrange("c p t -> p c t"),
                      reads=[RMRS], writes=[Rmr])
                p3[qb] = (xt, Rx, mr, Rmr)

            with ExitStack() as s2:
                sb, ps, ring = mk_alloc(s2)
                KTh = sb("KTh", [128, NKS * 128], BF16)
                Vh = sb("Vh", [128, NKS * 128], BF16)
                RKTh, RVh = Res("KTh"), Res("Vh")
                HB = []
                for p_ in range(2):
                    HB.append(dict(QTh=sb(f"QTh{p_}", [128, NQ * 128], BF16), GTh=sb(f"GTh{p_}", [128, NQ * 128], BF16),
                                   KSh=sb(f"KSh{p_}", [128, 256], BF16), VSh=sb(f"VSh{p_}", [128, 256], BF16),
                                   RQTh=Res(f"QTh{p_}"), RGTh=Res(f"GTh{p_}"), RKSh=Res(f"KSh{p_}"), RVSh=Res(f"VSh{p_}"),
                                   dq=S.dsem(f"hq{p_}"), dg=S.dsem(f"hg{p_}"), dks=S.dsem(f"hks{p_}"), dvs=S.dsem(f"hvs{p_}")))
                d_hk, d_hv = S.dsem("hk"), S.dsem("hv")
                ckb = ring("ckb", [128, 1024], BF16, 4, dsem=True)
                cvb = ring("cvb", [128, 1024], BF16, 4, dsem=True)
                Er = ring("E", [128, 1024], BF16, 5)
                wr = ring("w512", [128, 512], F32, 6)
                accr = ring("acc", [128, 512], F32, 4)
                hlr = ring("hl", [128, 512], BF16, 4)
                ocr = ring("oc", [128, 512], F32, 6)
                pSTr = ring("pST", [128, 1024], F32, 2, psum=True)
                pO = [ps(f"pO{c}", [128, 512], F32) for c in range(2)]
                RpO = [Res(f"pO{c}", excl=True) for c in range(2)]
                pS0, RpS0 = ps("pS0", [128, 512], F32), Res("pS0", excl=True)
                pF, RpF = ps("pF", [128, 512], F32), Res("pF", excl=True)

                deferred = []

                def flush():
                    while deferred:
                        deferred.pop(0)[1]()

                def attn_tile(h, blocks, q0, N, qb0, hb):
                    QTh, RQTh, GTh, RGTh = hb["QTh"], hb["RQTh"], hb["GTh"], hb["RGTh"]
                    acc, Racc = accr.next()
                    acco, Racco = accr.next()
                    po(lambda e: e.memset(acco[:, 0:N], 0.0), [], [Racco])
                    started = [False, False, False]

                    def stageA(bk, first, odd):
                        cs, ce = bk["cs"], bk["ce"]
                        n = ce - cs
                        pst, Rpst = pSTr.next()
                        subs = bk["subs"]
                        for si, sbk in enumerate(subs):
                            lo, hi = sbk["lo"], sbk["hi"]
                            pe(lambda e, sbk=sbk, lo=lo, hi=hi: e.matmul(pst[:, lo - cs:hi - cs], lhsT=sbk["kt"][0:64, :], rhs=QTh[0:64, q0 + lo:q0 + hi],
                                                                          start=True, stop=True), [sbk["RK"], RQTh], [Rpst], sig=False)
                            pe(lambda e, sbk=sbk, lo=lo, hi=hi: e.matmul(pst[:, 512 + lo - cs:512 + hi - cs], lhsT=sbk["kt"][64:128, :],
                                                                          rhs=QTh[64:128, q0 + lo:q0 + hi], start=True, stop=True),
                               [sbk["RK"], RQTh], [Rpst], sig=(si == len(subs) - 1))
                        E, RE = Er.next()
                        E3 = E[:].rearrange("p (c n) -> p c n", c=2)
                        p3 = pst[:].rearrange("p (c n) -> p c n", c=2)
                        ac(lambda e: e.activation(out=E3[:, :, 0:n], in_=p3[:, :, 0:n], func=AF.Exp, scale=0.125), [Rpst], [RE])
                        for prt, cols in bk["masks"]:
                            dv(lambda e, prt=prt, cols=cols: e.memset(E3[prt, :, cols], 0.0), [], [RE])
                        if first:
                            assert cs == 0 and ce == N
                            if bk["vmask"]:
                                dv(lambda e: e.tensor_scalar(out=acc[:, 0:n], in0=E[:, 512:512 + n], scalar1=vcol[:, 0:1], scalar2=None, op0=ALU.mult),
                                   [RE, Rc], [Racc])
                            else:
                                dv(lambda e: e.tensor_copy(acc[:, 0:n], E[:, 512:512 + n]), [RE], [Racc])
                        else:
                            a_, Ra = (acco, Racco) if odd else (acc, Racc)
                            dv(lambda e: e.tensor_tensor(out=a_[:, cs:ce], in0=a_[:, cs:ce], in1=E[:, 512:512 + n], op=ALU.add), [Ra, RE], [Ra])
                        return (bk, E, RE, first)

                    def stageB(item, last):
                        bk, E, RE, first = item
                        cs, ce = bk["cs"], bk["ce"]
                        n = ce - cs
                        subs = bk["subs"]
                        for c in range(2):
                            for si, sbk in enumerate(subs):
                                lo, hi = sbk["lo"], sbk["hi"]
                                st_ = not started[c]
                                started[c] = True
                                pe(lambda e, sbk=sbk, lo=lo, hi=hi, c=c, st_=st_: e.matmul(pO[c][:, lo:hi], lhsT=sbk["v"],
                                                                                          rhs=E[:, c * 512 + lo - cs:c * 512 + hi - cs],
                                                                                          start=st_, stop=last, skip_group_check=True),
                                   [sbk["RV"], RE], [RpO[c]], sig=(c == 1 and si == len(subs) - 1))
                            if c == 0:
                                st_ = not started[2]
                                started[2] = True
                                pe(lambda e, st_=st_: e.matmul(pS0[:, cs:ce], lhsT=(validb[:] if bk["vmask"] else onesb[:]), rhs=E[:, 0:n], start=st_,
                                                               stop=last, skip_group_check=True), [Rc, RE], [RpS0], sig=False)

                    nbk = len(blocks)
                    sched = [(int(f * nbk), fn) for f, fn in deferred]
                    del deferred[:]
                    pend = []
                    for i, bk in enumerate(blocks):
                        pend.append(stageA(bk, i == 0, i % 2 == 1))
                        if len(pend) > 2:
                            stageB(pend.pop(0), False)
                        while sched and sched[0][0] <= i:
                            sched.pop(0)[1]()
                    while pend:
                        it_ = pend.pop(0)
                        stageB(it_, len(pend) == 0)
                    while sched:
                        sched.pop(0)[1]()
                    o0, Ro0 = ocr.next()
                    o1, Ro1 = ocr.next()
                    s0, Rs0 = ocr.next()
                    dv(lambda e: e.tensor_copy(o0[:, 0:N], pO[0][:, 0:N]), [RpO[0]], [Ro0])
                    ac(lambda e: e.copy(out=s0[:, 0:N], in_=pS0[:, 0:N]), [RpS0], [Rs0])
                    dv(lambda e: e.tensor_copy(o1[:, 0:N], pO[1][:, 0:N]), [RpO[1]], [Ro1])
                    hi, Rhi = hlr.next()
                    lo, Rlo = hlr.next()
                    do, Rdo = wr.next()
                    s1, Rs1 = wr.next()
                    rs_, Rrs = wr.next()

                    def D1():
                        dv(lambda e: e.tensor_tensor(out=acc[:, 0:N], in0=acc[:, 0:N], in1=acco[:, 0:N], op=ALU.add), [Racc, Racco], [Racc])
                        po(lambda e: e.tensor_copy(hi[:, 0:N], acc[:, 0:N]), [Racc], [Rhi])

                    def D2():
                        pe(lambda e: e.matmul(pF[:, 0:N], lhsT=onesb[:], rhs=hi[:, 0:N], start=True, stop=True), [Rc, Rhi], [RpF])

                    def D2b():
                        ac(lambda e: e.copy(out=s1[:, 0:N], in_=pF[:, 0:N]), [RpF], [Rs1])

                    def mkR(t_, Rt, q):
                        def R():
                            dv(lambda e: e.reciprocal(out=t_[:, q * 128:(q + 1) * 128], in_=t_[:, q * 128:(q + 1) * 128]), [Rt], [Rt])
                        return R

                    Rs = [mkR(s0, Rs0, q) for q in range(N // 128)] + [mkR(s1, Rs1, q) for q in range(N // 128)]

                    def D3():
                        po(lambda e: e.tensor_tensor(out=o0[:, 0:N], in0=o0[:, 0:N], in1=s0[:, 0:N], op=ALU.mult), [Ro0, Rs0], [Ro0])
                        po(lambda e: e.tensor_tensor(out=o1[:, 0:N], in0=o1[:, 0:N], in1=s1[:, 0:N], op=ALU.mult), [Ro1, Rs1], [Ro1])
                        dv(lambda e: e.scalar_tensor_tensor(out=do[:, 0:N], in0=o1[:, 0:N], scalar=nlam[:, 0:1], in1=o0[:, 0:N],
                                                            op0=ALU.mult, op1=ALU.add), [Ro0, Ro1, Rc], [Rdo])
                        po(lambda e: e.tensor_tensor(out=hi[:, 0:N], in0=do[:, 0:N], in1=do[:, 0:N], op=ALU.mult), [Rdo], [Rhi])

                    def D4():
                        pe(lambda e: e.matmul(pF[:, 0:N], lhsT=onesb[:], rhs=hi[:, 0:N], start=True, stop=True), [Rc, Rhi], [RpF])

                    def D5():
                        ac(lambda e: e.activation(out=rs_[:, 0:N], in_=pF[:, 0:N], func=AF.Ln, bias=epsc[:, 0:1], scale=1.0 / 128), [RpF, Rc], [Rrs])
                        ac(lambda e: e.activation(out=rs_[:, 0:N], in_=rs_[:, 0:N], func=AF.Exp, scale=-0.5), [Rrs], [Rrs])
                        po(lambda e: e.tensor_tensor(out=do[:, 0:N], in0=do[:, 0:N], in1=rs_[:, 0:N], op=ALU.mult), [Rdo, Rrs], [Rdo])
                        nb = N // 128
                        dv(lambda e: e.scalar_tensor_tensor(out=MD3[:, h, q0:q0 + N], in0=do[:, 0:N], scalar=gd08[:, 0:1], in1=GTh[:, q0:q0 + N],
                                                            op0=ALU.mult, op1=ALU.mult), [Rdo, Rc, RGTh], [RMD[h][qb0 + i] for i in range(nb)])

                    stages = [(0.03, D1), (0.12, D2), (0.2, D2b)] + [(0.24 + 0.035 * k, R) for k, R in enumerate(Rs)] + [(0.56, D3), (0.76, D4), (0.9, D5)]
                    deferred.extend(stages)

                def head_loads(h):
                    B = HB[h % 2]
                    S.dma(S.sp, d_hk, KTh[:], KT[h], reads=[RKT], writes=[RKTh])
                    S.dma(S.sp, d_hv, Vh[:], VS[h].rearrange("p k e -> p (k e)"), reads=[RVS], writes=[RVh])
                    S.dma(S.sp, B["dq"], B["QTh"][:], QT[h], reads=[RQT], writes=[B["RQTh"]])
                    S.dma(S.sp, B["dg"], B["GTh"][:], GT[h], reads=[RGT], writes=[B["RGTh"]])
                    S.dma(S.sp, B["dks"], B["KSh"][:], KTS[h], reads=[RKT], writes=[B["RKSh"]])
                    S.dma(S.sp, B["dvs"], B["VSh"][:].rearrange("p (k e) -> p k e", e=128), VSS[:, :, h * 128:(h + 1) * 128].rearrange("k p e -> p k e"),
                          reads=[RVS], writes=[B["RVSh"]])

                head_loads(0)
                S.dma(S.sp, d_c, gf[:], gf_d[:, :], writes=[Rgf])
                p3_load(0)
                p3_load(1)
                for h in range(4):
                    B = HB[h % 2]
                    KSh, VSh, RKSh, RVSh = B["KSh"], B["VSh"], B["RKSh"], B["RVSh"]
                    for T in range(NT):
                        blocks = []
                        for ks in range(8 * T + 8):
                            m = ks - 8 * T
                            cs = 0 if m <= 0 else 128 * (m // 2)
                            masks = []
                            if m >= 1 and m % 2 == 1:
                                masks.append((slice(64, 128), slice(0, 64)))
                            blocks.append(dict(subs=[dict(kt=KTh[:, ks * 128:(ks + 1) * 128], RK=RKTh, v=Vh[:, ks * 128:(ks + 1) * 128], RV=RVh,
                                                          lo=cs, hi=512)], vmask=(ks == 0), cs=cs, ce=512, masks=masks))
                        attn_tile(h, blocks, T * 512, 512, 4 * T, B)
                        if T == min(1, NT - 1):
                            cks, cvs = [], []
                            for sidx in range(4):
                                kb_, Rkb, dkb_ = ckb.next()
                                vb_, Rvb, dvb_ = cvb.next()
                                S.dma(S.pool, dkb_, kb_[:], ckT_d[sidx, h], writes=[Rkb])
                                S.dma(S.pool, dvb_, vb_[:].rearrange("p (k e) -> p k e", e=128),
                                      cv_d[sidx].rearrange("(k p) e -> p k e", p=128)[:, :, h * 128:(h + 1) * 128], writes=[Rvb])
                                cks.append((kb_, Rkb))
                                cvs.append((vb_, Rvb))
                    if h + 1 < 4:
                        head_loads(h + 1)
                    blocks = [dict(subs=[dict(kt=KSh[:, b * 128:(b + 1) * 128], RK=RKSh, v=VSh[:, b * 128:(b + 1) * 128], RV=RVSh,
                                              lo=b * 128, hi=(b + 1) * 128) for b in range(2)],
                                   vmask=False, cs=0, ce=256,
                                   masks=[(slice(0, 64), slice(64, 128)), (slice(64, 128), slice(0, 64)),
                                          (slice(0, 64), slice(192, 256)), (slice(64, 128), slice(128, 192))])]
                    for kk in range(8):
                        blocks.append(dict(subs=[dict(kt=cks[s_][0][:, kk * 128:(kk + 1) * 128], RK=cks[s_][1],
                                                      v=cvs[s_][0][:, kk * 128:(kk + 1) * 128], RV=cvs[s_][1], lo=s_ * 64, hi=s_ * 64 + 64)
                                                 for s_ in range(4)], vmask=False, cs=0, ce=256, masks=[]))
                    attn_tile(h, blocks, NW * 128, 256, NW, B)
                flush()
                S.barrier()
                ckpt("p2")

            with ExitStack() as s3:
                sb, ps, ring = mk_alloc(s3)
                xsr = ring("xs3", [128, 1024], F32, 2)
                yor = ring("yo3", [128, 1024], F32, 2, dsem=True)
                junk3 = sb("junk3", [128, 1024], BF16)
                Rj3 = Res("junk3")
                ss3 = ring("ss3", [128, 2], F32, 2)
                pyr = ring("py", [128, 512], F32, 4, psum=True)
                for qb in range(NQ):
                    smp = qb >= NW
                    dst = y_smp[qb - NW] if smp else y_own[qb]
                    if qb + 2 < NQ:
                        p3_load(qb + 2)
                    xt, Rx, mr, Rmr = p3[qb]
                    xs, Rxs = xsr.next()
                    for n in range(2):
                        py, Rpy = pyr.next()
                        for c in range(8):
                            lhsT = mr[:, c * 128:(c + 1) * 128] if c < 4 else MD3[:, c - 4, qb * 128:(qb + 1) * 128]
                            Rl = Rmr if c < 4 else RMD[c - 4][qb]
                            pe(lambda e, lhsT=lhsT, c=c, n=n, py=py: e.matmul(py[:], lhsT=lhsT, rhs=woutb[:, c * 1024 + n * 512: c * 1024 + (n + 1) * 512],
                                                                              start=(c == 0), stop=(c == 7)), [Rl, Rwout], [Rpy], sig=(c == 7))
                        dv(lambda e, n=n, py=py: e.tensor_tensor(out=xs[:, n * 512:(n + 1) * 512], in0=py[:], in1=xt[:, n * 512:(n + 1) * 512], op=ALU.add),
                           [Rpy, Rx], [Rxs])
                    ss, Rss = ss3.next()
                    ac(lambda e: e.activation(out=junk3[:], in_=xs[:], func=AF.Square, accum_out=ss[:, 0:1]), [Rxs], [Rj3, Rss])
                    ac(lambda e: e.activation(out=ss[:, 1:2], in_=ss[:, 0:1], func=AF.Sqrt, bias=epsc[:, 0:1], scale=1.0 / 1024), [Rss, Rc], [Rss])
                    dv(lambda e: e.reciprocal(out=ss[:, 1:2], in_=ss[:, 1:2]), [Rss], [Rss])
                    yo, Ryo, dyo = yor.next()
                    dv(lambda e: e.scalar_tensor_tensor(out=yo[:], in0=xs[:], scalar=ss[:, 1:2], in1=gf[:], op0=ALU.mult, op1=ALU.mult), [Rxs, Rss, Rgf], [Ryo])
                    S.dma(S.pool, dyo, dst, yo[:], reads=[Ryo])
                S.barrier()
    return nc


def _tables(pos):
    pos = np.asarray(pos, np.float32)
    t = np.zeros((128, TABW), np.float32)
    inv_r = (1.0 / (np.float32(10000.0) ** (np.arange(0, 64, 2, dtype=np.float32) / np.float32(64)))).astype(np.float32)
    ang = (pos[:, None] * inv_r[None, :]).astype(np.float32)
    c, s = np.cos(ang), np.sin(ang)
    t[:, 0:32] = c
    t[:, 32:64] = c
    t[:, 64:96] = -s
    t[:, 96:128] = s
    inv_d = (1.0 / (np.float32(500000.0) ** (np.arange(0, 16, 2, dtype=np.float32) / np.float32(16)))).astype(np.float32)
    ang = (pos[:, None] * inv_d[None, :]).astype(np.float32)
    c, s = np.cos(ang), np.sin(ang)
    t[:, 128:136] = c
    t[:, 136:144] = c
    t[:, 144:192] = 1.0
    t[:, 192:200] = -s
    t[:, 200:208] = s
    return t


def _decay_consts(L):
    lg = np.log1p(-np.exp2(-5.0 - np.arange(8, dtype=np.float64)))
    p = np.arange(128)
    loc = p % L
    blk = p // L
    rel = loc[None, :] - loc[:, None]
    ok = (rel >= 0) & (blk[None, :] == blk[:, None])
    dm = np.zeros((128, 8, 128), np.float64)
    for h in range(8):
        dm[:, h, :] = np.where(ok, np.exp(lg[h] * np.maximum(rel, 0)), 0.0)
    dect = np.zeros((128, 20), np.float64)
    dect[:, 0:8] = np.exp(lg[None, :] * (loc[:, None] + 1.0))
    dect[:, 8:16] = np.exp(lg[None, :] * (L - 1.0 - loc[:, None]))
    for j in range(4):
        dect[0:64, 16 + j] = np.exp(lg[2 * j] * L)
        dect[64:128, 16 + j] = np.exp(lg[2 * j + 1] * L)
    return dm.reshape(128, 1024).astype(np.float32), dect.astype(np.float32)


_NC_CACHE = {}
_STOP = None


def _run(inputs, NT):
    f = lambda a: np.ascontiguousarray(np.asarray(a, dtype=np.float32))
    xp = f(inputs["x_prompt"])
    xs = f(inputs["x_sample"])
    ck = f(inputs["cache_k"])[0]
    cv = f(inputs["cache_v"])[0]
    sr = f(inputs["state_ret"])[0]
    B, L = xp.shape[0], xp.shape[1]
    assert L == 1024 * NT and B == 4
    NW, NO = 4 * NT, 4 * NT + 1
    past = ck.shape[1]
    dm_p, dc_p = _decay_consts(128)
    dm_s, dc_s = _decay_consts(64)
    rep = lambda v, n=128: np.ascontiguousarray(np.broadcast_to(np.asarray(v, np.float32).reshape(1, -1), (n, np.asarray(v).size)))
    lamv = np.concatenate([f(inputs[k])[0] for k in ("lam_q1", "lam_k1", "lam_q2", "lam_k2")])
    common = {
        "w_in": f(inputs["w_in"])[0], "w_out": f(inputs["w_out"])[0],
        "gn": rep(f(inputs["norm_g"])[0]), "gf": rep(f(inputs["final_norm_g"])), "gr": rep(f(inputs["ret_norm_g"])[0]),
        "gd": f(inputs["diff_norm_g"])[0].reshape(128, 1).copy(), "lamv": rep(lamv),
        "ident": np.eye(128, dtype=np.float32), "dm_p": dm_p, "dm_s": dm_s, "dect_p": dc_p, "dect_s": dc_s,
    }
    t_smp_blk = _tables(past + (np.arange(128) % 64))
    in_maps = []
    for c in range(8):
        b, s = c // 2, c % 2
        xb = xp[b].reshape(2 * NW, 128, 1024)
        own_blocks = [2 * i + s for i in range(NW)]
        oth_blocks = [(2 * i if s == 1 else 2 * i - 1) for i in range(NO)]
        x_own = xb[own_blocks]
        x_oth = np.zeros((NO, 128, 1024), np.float32)
        t_oth = np.zeros((NO, 128, TABW), np.float32)
        for i, blk in enumerate(oth_blocks):
            if 0 <= blk < 2 * NW:
                x_oth[i] = xb[blk]
                t_oth[i] = _tables(blk * 128 + np.arange(128))
            else:
                t_oth[i] = _tables(np.zeros(128))
        t_own = np.stack([_tables(blk * 128 + np.arange(128)) for blk in own_blocks])
        streams = [4 * c + r for r in range(4)]
        m = dict(common)
        m.update({
            "x_oth": x_oth, "x_own": np.ascontiguousarray(x_own), "x_smp": np.ascontiguousarray(xs[streams].reshape(2, 128, 1024)),
            "t_oth": t_oth, "t_own": t_own, "t_smp": np.stack([t_smp_blk, t_smp_blk]),
            "valid0": (np.ones((128, 128), np.float32) if s == 1 else np.zeros((128, 128), np.float32)),
            "cache_kT": np.ascontiguousarray(ck[streams].reshape(4, past, 4, 128).transpose(0, 2, 3, 1)),
            "cache_v": np.ascontiguousarray(cv[streams].reshape(4, past, 512)),
            "state_in": np.ascontiguousarray(sr[streams]),
        })
        in_maps.append(m)
    if NT not in _NC_CACHE:
        _NC_CACHE[NT] = build(NT, _STOP)
    nc = _NC_CACHE[NT]
    res = run_bass_kernel_spmd(nc, in_maps, core_ids=list(range(8)))
    R = res.results
    y_p = np.zeros((4, 2 * NW, 128, 1024), np.float32)
    k_p = np.zeros((4, 2 * NW, 128, 512), np.float32)
    v_p = np.zeros((4, 2 * NW, 128, 512), np.float32)
    st_p = np.zeros((4, 8, 64, 64), np.float32)
    y_s = np.zeros((32, 64, 1024), np.float32)
    k_s = np.zeros((32, 64, 512), np.float32)
    v_s = np.zeros((32, 64, 512), np.float32)
    st_s = np.zeros((32, 8, 64, 64), np.float32)
    for c in range(8):
        b, s = c // 2, c % 2
        r = R[c]
        y_p[b, s::2] = r["y_own"]
        k_p[b, s::2] = r["k_own"]
        v_p[b, s::2] = r["v_own"]
        if s == 0:
            st_p[b] = r["st_p"]
        y_s[4 * c:4 * c + 4] = r["y_smp"].reshape(4, 64, 1024)
        k_s[4 * c:4 * c + 4] = r["k_smp"].reshape(4, 64, 512)
        v_s[4 * c:4 * c + 4] = r["v_smp"].reshape(4, 64, 512)
        st_s[4 * c:4 * c + 4] = r["st_s"]
    return (y_p.reshape(4, L, 1024), y_s, st_p[None], st_s[None], k_p.reshape(1, 4, L, 4, 128), v_p.reshape(1, 4, L, 4, 128),
            k_s.reshape(1, 32, 64, 4, 128), v_s.reshape(1, 32, 64, 4, 128))


def kernel(**inputs):
    return _run(inputs, 8)
```

```python
import math
from contextlib import ExitStack

import numpy as np
import concourse.bass as bass
import concourse.mybir as mybir
from concourse.bass_utils import run_bass_kernel_spmd

F32 = mybir.dt.float32
BF16 = mybir.dt.bfloat16
AF = mybir.ActivationFunctionType
ALU = mybir.AluOpType
AX = mybir.AxisListType
EPS = 1e-6
TABW = 208


class Res:
    __slots__ = ("name", "w", "r", "acc", "excl")

    def __init__(self, name, acc=False, excl=False):
        self.name = name
        self.w = {}
        self.r = {}
        self.acc = acc
        self.excl = excl


class Eng:
    def __init__(self, nc, e, name, st):
        self.e = e
        self.name = name
        self.sem = st.enter_context(nc.semaphore("s_" + name))
        self.cnt = 0
        self.seen = {}


class DSem:
    def __init__(self, nc, name, st):
        self.name = "d_" + name
        self.sem = st.enter_context(nc.semaphore("d_" + name))
        self.cnt = 0


class Sched:
    def __init__(self, nc, st):
        self.nc = nc
        self.st = st
        self.pe = Eng(nc, nc.tensor, "pe", st)
        self.dve = Eng(nc, nc.vector, "dve", st)
        self.act = Eng(nc, nc.scalar, "act", st)
        self.pool = Eng(nc, nc.gpsimd, "pool", st)
        self.sp = Eng(nc, nc.sync, "sp", st)
        self.engs = [self.pe, self.dve, self.act, self.pool, self.sp]
        self.dsems = []

    def dsem(self, name):
        d = DSem(self.nc, name, self.st)
        self.dsems.append(d)
        return d

    def _deps(self, eng, reads, writes):
        deps = {}

        def add(d):
            for key, (sem, val) in d.items():
                if key == "pe" and eng.name == "pe":
                    continue
                if key not in deps or deps[key][1] < val:
                    deps[key] = (sem, val)

        for r in reads:
            add(r.w)
            if r.excl:
                add({k: v for k, v in r.r.items() if k != eng.name})
        for w in writes:
            add(w.w)
            add(w.r)
        for key, (sem, val) in deps.items():
            if eng.seen.get(key, 0) >= val:
                continue
            eng.e.wait_ge(sem, val)
            eng.seen[key] = val

    def _mark(self, tick, reads, writes):
        key, sem, val = tick
        for r in reads:
            if key not in r.r or r.r[key][1] < val:
                r.r[key] = (sem, val)
        for w in writes:
            if w.acc:
                w.w[key] = (sem, val)
            else:
                w.w = {key: (sem, val)}
            w.r = {}

    def op(self, eng, fn, reads=(), writes=(), signal=True):
        self._deps(eng, reads, writes)
        ins = fn(eng.e)
        tick = (eng.name, eng.sem, eng.cnt + 1)
        if signal:
            ins.then_inc(eng.sem, 1)
            eng.cnt += 1
        else:
            assert eng.name == "pe"
        self._mark(tick, reads, writes)

    def dma(self, q, dsem, out, in_, reads=(), writes=()):
        self._deps(q, reads, writes)
        q.e.dma_start(out=out, in_=in_).then_inc(dsem.sem, 16)
        dsem.cnt += 16
        self._mark((dsem.name, dsem.sem, dsem.cnt), reads, writes)

    def barrier(self):
        for e in self.engs:
            for o in self.engs:
                if o is e or o.cnt == 0:
                    continue
                if e.seen.get(o.name, 0) < o.cnt:
                    e.e.wait_ge(o.sem, o.cnt)
                    e.seen[o.name] = o.cnt
            for d in self.dsems:
                if d.cnt and e.seen.get(d.name, 0) < d.cnt:
                    e.e.wait_ge(d.sem, d.cnt)
                    e.seen[d.name] = d.cnt


class Ring:
    def __init__(self, items):
        self.items = items
        self.i = 0

    def next(self):
        it = self.items[self.i % len(self.items)]
        self.i += 1
        return it


class _Stop(Exception):
    pass


def build(NT, stop=None):
    nc = bass.Bass("TRN2", target_bir_lowering=False)
    try:
        _emit(nc, NT, stop)
    except _Stop:
        pass
    return nc


def _emit(nc, NT, stop):
    NW = 4 * NT
    NO = NW + 1
    NKS = NW + NO
    NQ = NW + 2
    def din(name, shape):
        return nc.dram_tensor(name, shape, F32, kind="ExternalInput").ap()

    def dout(name, shape):
        return nc.dram_tensor(name, shape, F32, kind="ExternalOutput").ap()

    def dscr(name, shape):
        return nc.dram_tensor(name, shape, BF16, kind="Internal").ap()

    x_oth = din("x_oth", [NO, 128, 1024])
    x_own = din("x_own", [NW, 128, 1024])
    x_smp = din("x_smp", [2, 128, 1024])
    t_oth = din("t_oth", [NO, 128, TABW])
    t_own = din("t_own", [NW, 128, TABW])
    t_smp = din("t_smp", [2, 128, TABW])
    w_in = din("w_in", [1024, 4096])
    w_out = din("w_out", [1024, 1024])
    gn_d = din("gn", [128, 1024])
    gf_d = din("gf", [128, 1024])
    gr_d = din("gr", [128, 512])
    gd_d = din("gd", [128, 1])
    lam_d = din("lamv", [128, 256])
    id_d = din("ident", [128, 128])
    dmp_d = din("dm_p", [128, 1024])
    dms_d = din("dm_s", [128, 1024])
    dcp_d = din("dect_p", [128, 20])
    dcs_d = din("dect_s", [128, 20])
    val_d = din("valid0", [128, 128])
    ckT_d = din("cache_kT", [4, 4, 128, 1024])
    cv_d = din("cache_v", [4, 1024, 512])
    sin_d = din("state_in", [4, 8, 64, 64])

    y_own = dout("y_own", [NW, 128, 1024])
    y_smp = dout("y_smp", [2, 128, 1024])
    k_own = dout("k_own", [NW, 128, 512])
    v_own = dout("v_own", [NW, 128, 512])
    k_smp = dout("k_smp", [2, 128, 512])
    v_smp = dout("v_smp", [2, 128, 512])
    st_p = dout("st_p", [8, 64, 64])
    st_s = dout("st_s", [4, 8, 64, 64])

    KT = dscr("KT", [4, 128, NKS * 128])
    VS = dscr("VS", [4, 128, NKS, 128])
    KTS = dscr("KTS", [4, 128, 256])
    VSS = dscr("VSS", [2, 128, 512])
    QT = dscr("QT", [4, 128, NQ * 128])
    GT = dscr("GT", [4, 128, NQ * 128])
    MRS = dscr("MRS", [4, 128, NQ * 128])

    with ExitStack() as st:
        S = Sched(nc, st)

        def ckpt(name):
            if stop == name:
                S.barrier()
                raise _Stop()

        def dv(fn, R, W):
            S.op(S.dve, fn, R, W)

        def ac(fn, R, W):
            S.op(S.act, fn, R, W)

        def po(fn, R, W):
            S.op(S.pool, fn, R, W)

        def pe(fn, R, W, sig=True):
            S.op(S.pe, fn, R, W, signal=sig)

        def mk_alloc(stack):
            def sb(name, shape, dt=F32):
                return stack.enter_context(nc.sbuf_tensor("sb_" + name, shape, dt))

            def ps(name, shape, dt=F32):
                return stack.enter_context(nc.psum_tensor("ps_" + name, shape, dt))

            def ring(name, shape, dt, n, dsem=False, psum=False):
                items = []
                for i in range(n):
                    t = (ps if psum else sb)(f"{name}{i}", shape, dt)
                    it = (t, Res(f"{name}{i}", excl=psum))
                    if dsem:
                        it = it + (S.dsem(f"{name}{i}"),)
                    items.append(it)
                return Ring(items)

            return sb, ps, ring

        sb0, ps0, ring0 = mk_alloc(st)

        woutb = sb0("woutb", [128, 8 * 1024], BF16)
        identb = sb0("identb", [128, 128], BF16)
        onesb = sb0("onesb", [128, 128], BF16)
        validb = sb0("validb", [128, 128], BF16)
        epsc = sb0("epsc", [128, 1])
        vcol = sb0("vcol", [128, 1])
        nlam = sb0("nlam", [128, 1])
        gd08 = sb0("gd08", [128, 1])
        Rc = Res("consts")
        Rc.acc = True
        Rwout = Res("woutb", acc=True)
        RMRS = Res("MRS", acc=True)
        RMD = [[Res(f"MD{h}_{i}") for i in range(NQ)] for h in range(4)]
        d_c = S.dsem("cst")
        d_scr = S.dsem("scr")
        d_out = S.dsem("outm")
        RKT = Res("KT", acc=True)
        RVS = Res("VS", acc=True)
        RQT = Res("QT", acc=True)
        RGT = Res("GT", acc=True)

        with ExitStack() as s1:
            sb, ps, ring = mk_alloc(s1)
            winb = sb("winb", [128, 8 * 4096], BF16)
            Rwin = Res("winb", acc=True)
            gn = sb("gn", [128, 1024])
            gr = sb("gr", [128, 512])
            dm_p = sb("dm_p", [128, 1024])
            dm_s = sb("dm_s", [128, 1024])
            dc_p = sb("dc_p", [128, 20])
            dc_s = sb("dc_s", [128, 20])
            idf = sb("idf", [128, 128])
            valf = sb("valf", [128, 128])
            lamt = sb("lamt", [128, 256])
            lamp = sb("lamp", [128, 256])
            lams = sb("lams", [128, 4])
            gdt = sb("gdt", [128, 1])
            Rtmp = Res("setup_tmp")

            for t, d in ((gn, gn_d), (gr, gr_d), (dm_p, dmp_d), (dm_s, dms_d), (dc_p, dcp_d), (dc_s, dcs_d),
                         (idf, id_d), (valf, val_d), (lamt, lam_d), (gdt, gd_d)):
                S.dma(S.sp, d_c, t[:], d[:, :], writes=[Rc])
            dv(lambda e: e.tensor_copy(identb[:], idf[:]), [Rc], [Rc])
            dv(lambda e: e.tensor_copy(validb[:], valf[:]), [Rc], [Rc])
            dv(lambda e: e.tensor_copy(vcol[:], valf[:, 0:1]), [Rc], [Rc])
            dv(lambda e: e.memset(onesb[:], 1.0), [], [Rc])
            dv(lambda e: e.memset(epsc[:], EPS), [], [Rc])
            lt4 = lamt[:].rearrange("p (a b d) -> p a b d", a=2, b=2)
            lp3 = lamp[:, 0:128].rearrange("p (a d) -> p a d", a=2)
            dv(lambda e: e.tensor_tensor(out=lp3, in0=lt4[:, :, 0, :], in1=lt4[:, :, 1, :], op=ALU.mult), [Rc], [Rtmp])
            dv(lambda e: e.tensor_reduce(out=lams[:, 0:2], in_=lp3, axis=AX.X, op=ALU.add), [Rtmp], [Rtmp])
            ac(lambda e: e.activation(out=lams[:, 2:4], in_=lams[:, 0:2], func=AF.Exp), [Rtmp], [Rtmp])
            dv(lambda e: e.tensor_tensor(out=lams[:, 0:1], in0=lams[:, 3:4], in1=lams[:, 2:3], op=ALU.subtract), [Rtmp], [Rtmp])
            lam_init = 0.8 - 0.6 * math.exp(-0.3 * 0)
            dv(lambda e: e.tensor_scalar(out=nlam[:], in0=lams[:, 0:1], scalar1=-lam_init, scalar2=None, op0=ALU.add), [Rtmp], [Rc])
            dv(lambda e: e.tensor_scalar(out=gd08[:], in0=gdt[:], scalar1=1.0 - lam_init, scalar2=None, op0=ALU.mult), [Rc], [Rc])

            RwinG = [Res(f"wing{g}", acc=True) for g in range(8)]
            d_wg = [S.dsem(f"wg{g}") for g in range(8)]
            for g in (1, 5, 2, 6, 0, 4, 3, 7):
                S.dma(S.pool, d_wg[g], winb[:].rearrange("p (kc c) -> p kc c", kc=8)[:, :, g * 512:(g + 1) * 512],
                      w_in[:, g * 512:(g + 1) * 512].rearrange("(kc p) c -> p kc c", p=128), writes=[RwinG[g]])
            d_wo = S.dsem("wo")

            def load_wout():
                S.dma(S.pool, d_wo, woutb[:].rearrange("p (kc c) -> p kc c", kc=8), w_out.rearrange("(kc p) c -> p kc c", p=128), writes=[Rwout])

            ckpt("setup")
            xring = ring("xt", [128, 1024], F32, 2, dsem=True)
            tring = ring("tb", [128, TABW], F32, 2, dsem=True)
            junk = sb("junk", [128, 1024], BF16)
            Rjunk = Res("junk")
            ssr = ring("ss", [128, 2], F32, 2)
            hbr = ring("hb", [128, 1024], BF16, 2)
            hTr = ring("hT", [128, 1024], BF16, 2)
            f32r = ring("f32t", [128, 512], F32, 6)
            kor = ring("ko", [128, 512], F32, 2, dsem=True)
            vor = ring("vo", [128, 512], F32, 2, dsem=True)
            ktr = ring("kts", [128, 512], BF16, 2, dsem=True)
            qtr = ring("qts", [128, 512], BF16, 2, dsem=True)
            gtr = ring("gts", [128, 512], BF16, 2, dsem=True)
            vbr = ring("vbs", [128, 512], BF16, 2, dsem=True)
            qkTr = ring("qkT", [128, 1024], BF16, 2)
            qab = sb("qab", [128, 1024], BF16)
            Rqab = Res("qab")
            sTm = sb("sTm", [128, 1024], BF16)
            RsTm = Res("sTm")
            t128 = sb("t128", [128, 128])
            Rt128 = Res("t128")
            ss8 = sb("ss8", [128, 8])
            Rss8 = Res("ss8")
            Sf = sb("Sf", [128, 256])
            Sb_ = sb("Sb", [128, 256], BF16)
            RSf, RSb = Res("Sf"), Res("Sb")
            d_stp = S.dsem("stp")

            pzr = ring("pz", [128, 512], F32, 3, psum=True)
            pT, RpT = ps("pT", [128, 1024], BF16), Res("pT", excl=True)
            pT2, RpT2 = ps("pT2", [128, 1024], BF16), Res("pT2", excl=True)
            pSc, RpSc = ps("pSc", [128, 512], F32), Res("pSc", excl=True)
            pIO, RpIO = ps("pIO", [128, 512], F32), Res("pIO", excl=True)
            pSt, RpSt = ps("pSt", [128, 512], F32), Res("pSt", excl=True)

            dv(lambda e: e.memset(Sf[:], 0.0), [], [RSf])
            dv(lambda e: e.memset(Sb_[:], 0.0), [], [RSb])

            xsrc = {"oth": x_oth, "own": x_own, "smp": x_smp}
            tsrc = {"oth": t_oth, "own": t_own, "smp": t_smp}

            def inproj(g, hT, RhT):
                pz_, Rpz = pzr.next()
                for kc in range(8):
                    pe(lambda e, kc=kc: e.matmul(pz_[:], lhsT=hT[:, kc * 128:(kc + 1) * 128],
                                                 rhs=winb[:, kc * 4096 + g * 512: kc * 4096 + (g + 1) * 512],
                                                 start=(kc == 0), stop=(kc == 7)),
                       [RhT, RwinG[g]], [Rpz], sig=(kc == 7))
                return pz_, Rpz

            def rot_ret(zp, Rz, tb, Rt, outb, Rout, scale):
                t1, Rt1 = f32r.next()
                t2, Rt2 = f32r.next()
                z3 = zp[:].rearrange("p (h d) -> p h d", h=8)
                z4 = zp[:].rearrange("p (h t d) -> p h t d", h=8, t=2)
                t13 = t1[:].rearrange("p (h d) -> p h d", h=8)
                t24 = t2[:].rearrange("p (h t d) -> p h t d", h=8, t=2)
                cosb = tb[:, 0:64].unsqueeze(1).to_broadcast([128, 8, 64])
                nsb = tb[:, 64:96].unsqueeze(1).to_broadcast([128, 8, 32])
                psb = tb[:, 96:128].unsqueeze(1).to_broadcast([128, 8, 32])
                dv(lambda e: e.scalar_tensor_tensor(out=t13, in0=z3, scalar=scale, in1=cosb, op0=ALU.mult, op1=ALU.mult), [Rz, Rt], [Rt1])
                dv(lambda e: e.scalar_tensor_tensor(out=t24[:, :, 0, :], in0=z4[:, :, 1, :], scalar=scale, in1=nsb, op0=ALU.mult, op1=ALU.mult), [Rz, Rt], [Rt2])
                dv(lambda e: e.scalar_tensor_tensor(out=t24[:, :, 1, :], in0=z4[:, :, 0, :], scalar=scale, in1=psb, op0=ALU.mult, op1=ALU.mult), [Rz, Rt], [Rt2])
                dv(lambda e: e.tensor_tensor(out=outb[:], in0=t1[:], in1=t2[:], op=ALU.add), [Rt1, Rt2], [Rout])

            def rot_diff(zp, Rz, tb, Rt, outf, Rout):
                z3 = zp[:].rearrange("p (g d) -> p g d", g=8)
                o3 = outf[:].rearrange("p (g d) -> p g d", g=8)
                t3 = t128[:].rearrange("p (g d) -> p g d", g=8)
                cosb = tb[:, 128:192].unsqueeze(1).to_broadcast([128, 8, 64])
                nsb = tb[:, 192:200].unsqueeze(1).to_broadcast([128, 8, 8])
                psb = tb[:, 200:208].unsqueeze(1).to_broadcast([128, 8, 8])
                dv(lambda e: e.tensor_tensor(out=o3, in0=z3, in1=cosb, op=ALU.mult), [Rz, Rt], [Rout])
                dv(lambda e: e.tensor_tensor(out=t3[:, :, 0:8], in0=z3[:, :, 8:16], in1=nsb, op=ALU.mult), [Rz, Rt], [Rt128])
                dv(lambda e: e.tensor_tensor(out=t3[:, :, 8:16], in0=z3[:, :, 0:8], in1=psb, op=ALU.mult), [Rz, Rt], [Rt128])
                dv(lambda e: e.tensor_tensor(out=o3[:, :, 0:16], in0=o3[:, :, 0:16], in1=t3, op=ALU.add), [Rout, Rt128], [Rout])

            def transp4(src, Rsrc, pdst, Rpd, off):
                for c in range(4):
                    pe(lambda e, c=c: e.transpose(pdst[:, off + c * 128: off + (c + 1) * 128], src[:, c * 128:(c + 1) * 128], identb[:]),
                       [Rsrc, Rc], [Rpd], sig=(c == 3))

            mrr = ring("mrs", [128, 512], BF16, 2, dsem=True)
            XS = []
            for p_ in range(3):
                d_ = {}
                for nm in ("Kb", "Kd", "Vb", "Qb", "dkb", "dqb"):
                    d_[nm] = sb(f"{nm}{p_}", [128, 512], BF16)
                    d_["R" + nm] = Res(f"{nm}{p_}")
                for nm in ("gate", "ro"):
                    d_[nm] = sb(f"{nm}{p_}", [128, 512])
                    d_["R" + nm] = Res(f"{nm}{p_}")
                XS.append(d_)
            tcr = ring("tc", [128, 512], F32, 2)
            sqt, Rsqt = sb("sqt", [128, 512]), Res("sqt")
            mixr = ring("mixb", [128, 512], BF16, 2)
            SfS2 = [[sb(f"SfS{b_}_{r}", [128, 256]) for r in range(2)] for b_ in range(2)]
            SbS2 = [[sb(f"SbS{b_}_{r}", [128, 256], BF16) for r in range(2)] for b_ in range(2)]
            RSfS2 = [[Res(f"SfS{b_}_{r}") for r in range(2)] for b_ in range(2)]
            RSbS2 = [[Res(f"SbS{b_}_{r}") for r in range(2)] for b_ in range(2)]
            d_st2 = [[S.dsem(f"sti{b_}{r}") for r in range(2)] for b_ in range(2)]
            d_sts2 = [[S.dsem(f"sto{b_}{r}") for r in range(2)] for b_ in range(2)]

            def P0(c):
                kind, idx = c["kind"], c["idx"]
                smp = kind == "smp"
                xt, Rx, dx = xring.next()
                tb, Rt, dtb = tring.next()
                S.dma(S.sp, dx, xt[:], xsrc[kind][idx], writes=[Rx])
                S.dma(S.sp, dtb, tb[:], tsrc[kind][idx], writes=[Rt])
                if smp:
                    for r in range(2):
                        for par in range(2):
                            src = sin_d[2 * idx + r].rearrange("(j par) d e -> par d j e", par=2)[par]
                            S.dma(S.sp, d_st2[idx][r], SfS2[idx][r][par * 64:(par + 1) * 64, :].rearrange("p (j e) -> p j e", j=4), src,
                                  writes=[RSfS2[idx][r]])
                        ac(lambda e, r=r: e.copy(out=SbS2[idx][r][:], in_=SfS2[idx][r][:]), [RSfS2[idx][r]], [RSbS2[idx][r]])
                ss, Rss = ssr.next()
                hb, Rhb = hbr.next()
                hT, RhT = hTr.next()
                ac(lambda e: e.activation(out=junk[:], in_=xt[:], func=AF.Square, accum_out=ss[:, 0:1]), [Rx], [Rjunk, Rss])
                ac(lambda e: e.activation(out=ss[:, 1:2], in_=ss[:, 0:1], func=AF.Sqrt, bias=epsc[:, 0:1], scale=1.0 / 1024), [Rss, Rc], [Rss])
                dv(lambda e: e.reciprocal(out=ss[:, 1:2], in_=ss[:, 1:2]), [Rss], [Rss])
                dv(lambda e: e.scalar_tensor_tensor(out=hb[:], in0=xt[:], scalar=ss[:, 1:2], in1=gn[:], op0=ALU.mult, op1=ALU.mult), [Rx, Rss, Rc], [Rhb])
                c.update(tb=tb, Rt=Rt, hT=hT, RhT=RhT, hb=hb, Rhb=Rhb)

            def P0b(c):
                hb, Rhb, hT, RhT = c["hb"], c["Rhb"], c["hT"], c["RhT"]
                for kc in range(8):
                    pe(lambda e, kc=kc: e.transpose(pT[:, kc * 128:(kc + 1) * 128], hb[:, kc * 128:(kc + 1) * 128], identb[:]),
                       [Rhb, Rc], [RpT], sig=(kc == 7))
                ac(lambda e: e.copy(out=hT[:], in_=pT[:]), [RpT], [RhT])

            def P1(c):
                kind, idx, ks = c["kind"], c["idx"], c["ks"]
                own = kind != "oth"
                smp = kind == "smp"
                qb = idx if kind == "own" else NW + idx
                dc = dc_s if smp else dc_p
                X = XS[c["seq"] % 3]
                tb, Rt, hT, RhT = c["tb"], c["Rt"], c["hT"], c["RhT"]
                zk, Rzk = inproj(1, hT, RhT)
                rot_ret(zk, Rzk, tb, Rt, X["Kb"], X["RKb"], 0.125)
                po(lambda e: e.tensor_tensor(out=X["Kd"][:].rearrange("p (h d) -> p h d", h=8), in0=X["Kb"][:].rearrange("p (h d) -> p h d", h=8),
                                             in1=dc[:, 8:16].unsqueeze(2).to_broadcast([128, 8, 64]), op=ALU.mult), [X["RKb"], Rc], [X["RKd"]])
                yield
                if own:
                    zq, Rzq = inproj(0, hT, RhT)
                    rot_ret(zq, Rzq, tb, Rt, X["Qb"], X["RQb"], 1.0)
                    yield
                zdk, Rzdk = inproj(5, hT, RhT)
                ko, Rko, dko = kor.next()
                rot_diff(zdk, Rzdk, tb, Rt, ko, Rko)
                if own:
                    S.dma(S.pool, dko, (k_smp if smp else k_own)[idx], ko[:], reads=[Rko])
                ac(lambda e: e.copy(out=X["dkb"][:], in_=ko[:]), [Rko], [X["Rdkb"]])
                yield
                if own:
                    zdq, Rzdq = inproj(4, hT, RhT)
                    qf, Rqf = f32r.next()
                    rot_diff(zdq, Rzdq, tb, Rt, qf, Rqf)
                    ac(lambda e: e.copy(out=X["dqb"][:], in_=qf[:]), [Rqf], [X["Rdqb"]])
                    yield
                zv, Rzv = inproj(2, hT, RhT)
                ac(lambda e: e.copy(out=X["Vb"][:], in_=zv[:]), [Rzv], [X["RVb"]])
                yield
                zdv, Rzdv = inproj(6, hT, RhT)
                vbs, Rvbs, dvbs = vbr.next()
                ac(lambda e: e.copy(out=vbs[:], in_=zdv[:]), [Rzdv], [Rvbs])
                if smp:
                    S.dma(S.pool, dvbs, VSS[idx], vbs[:], reads=[Rvbs], writes=[RVS])
                else:
                    S.dma(S.pool, dvbs, VS[:, :, ks, :].rearrange("h p e -> p h e"), vbs[:].rearrange("p (h e) -> p h e", h=4), reads=[Rvbs], writes=[RVS])
                if own:
                    vo, Rvo, dvo = vor.next()
                    dv(lambda e: e.tensor_copy(vo[:], zdv[:]), [Rzdv], [Rvo])
                    S.dma(S.pool, dvo, (v_smp if smp else v_own)[idx], vo[:], reads=[Rvo])
                yield
                if own:
                    zg, Rzg = inproj(3, hT, RhT)
                    ac(lambda e: e.activation(out=X["gate"][:], in_=zg[:], func=AF.Silu), [Rzg], [X["Rgate"]])
                    yield
                    pzg, Rpzg = pzr.next()
                    for h in range(4):
                        for kc in range(8):
                            pe(lambda e, h=h, kc=kc: e.matmul(pzg[:, h * 128:(h + 1) * 128],
                                                              lhsT=winb[:, kc * 4096 + 3584 + h * 128: kc * 4096 + 3584 + (h + 1) * 128],
                                                              rhs=hT[:, kc * 128:(kc + 1) * 128], start=(kc == 0), stop=(kc == 7)),
                               [RwinG[7], RhT], [Rpzg], sig=(h == 3 and kc == 7))
                    gts, Rgts, dgts = gtr.next()
                    ac(lambda e: e.activation(out=gts[:], in_=pzg[:], func=AF.Silu), [Rpzg], [Rgts])
                    S.dma(S.pool, dgts, GT[:, :, qb * 128:(qb + 1) * 128].rearrange("h p t -> p h t"),
                          gts[:].rearrange("p (h t) -> p h t", h=4), reads=[Rgts], writes=[RGT])
                    yield

            def P2(c):
                kind, idx, ks = c["kind"], c["idx"], c["ks"]
                own = kind != "oth"
                smp = kind == "smp"
                qb = idx if kind == "own" else NW + idx
                dm = dm_s if smp else dm_p
                dc = dc_s if smp else dc_p
                X = XS[c["seq"] % 3]
                Kb, RKb, Kd, RKd, Vb, RVb, Qb, RQb = X["Kb"], X["RKb"], X["Kd"], X["RKd"], X["Vb"], X["RVb"], X["Qb"], X["RQb"]
                dkb, Rdkb, dqb, Rdqb, gate, Rgate, ro, Rro = X["dkb"], X["Rdkb"], X["dqb"], X["Rdqb"], X["gate"], X["Rgate"], X["ro"], X["Rro"]
                if own:
                    transp4(Qb, RQb, pT2, RpT2, 0)
                    transp4(Kb, RKb, pT2, RpT2, 512)
                    qkT, RqkT = qkTr.next()
                    ac(lambda e: e.copy(out=qkT[:], in_=pT2[:]), [RpT2], [RqkT])
                    yield
                transp4(dkb, Rdkb, pT2, RpT2, 512)
                kts, Rkts, dkts = ktr.next()
                dv(lambda e: e.tensor_copy(kts[:], pT2[:, 512:1024]), [RpT2], [Rkts])
                dstK = KTS[:, :, idx * 128:(idx + 1) * 128] if smp else KT[:, :, ks * 128:(ks + 1) * 128]
                S.dma(S.pool, dkts, dstK.rearrange("h p t -> p h t"), kts[:].rearrange("p (h t) -> p h t", h=4), reads=[Rkts], writes=[RKT])
                yield
                if own:
                    transp4(dqb, Rdqb, pT2, RpT2, 0)
                    qts, Rqts, dqts = qtr.next()
                    dv(lambda e: e.tensor_copy(qts[:], pT2[:, 0:512]), [RpT2], [Rqts])
                    S.dma(S.pool, dqts, QT[:, :, qb * 128:(qb + 1) * 128].rearrange("h p t -> p h t"),
                          qts[:].rearrange("p (h t) -> p h t", h=4), reads=[Rqts], writes=[RQT])
                    yield
                    if smp:
                        SbS, RSbS = SbS2[idx], RSbS2[idx]
                        q3 = qkT[:, 0:512].rearrange("p (c t) -> p c t", c=4)
                        a3 = qab[:, 0:512].rearrange("p (c t) -> p c t", c=4)
                        b3 = qab[:, 512:1024].rearrange("p (c t) -> p c t", c=4)
                        po(lambda e: e.memset(qab[:], 0.0), [], [Rqab])
                        po(lambda e: e.tensor_copy(a3[:, :, 0:64], q3[:, :, 0:64]), [RqkT], [Rqab])
                        po(lambda e: e.tensor_copy(b3[:, :, 64:128], q3[:, :, 64:128]), [RqkT], [Rqab])
                    ro3 = ro[:].rearrange("p (h e) -> p h e", h=8)
                    for half in range(2):
                        for hh in range(4):
                            h = 4 * half + hh
                            j, p0 = h // 2, (h % 2) * 64
                            pb, Rpb = (pSc, RpSc) if hh % 2 == 0 else (pSt, RpSt)
                            pe(lambda e, hh=hh, j=j, p0=p0, pb=pb: e.matmul(pb[:, hh * 128:(hh + 1) * 128],
                                                                            lhsT=qkT[p0:p0 + 64, 512 + j * 128: 512 + (j + 1) * 128],
                                                                            rhs=qkT[p0:p0 + 64, j * 128:(j + 1) * 128], start=True, stop=True),
                               [RqkT], [Rpb], sig=(hh >= 2))
                        for b_ in range(2):
                            pb, Rpb = (pSc, RpSc) if b_ == 0 else (pSt, RpSt)
                            v4 = lambda ap, b_=b_: ap.rearrange("p (a b t) -> p a b t", a=2, b=2)[:, :, b_, :]
                            dv(lambda e, half=half, pb=pb, v4=v4: e.tensor_tensor(out=v4(sTm[:, half * 512:(half + 1) * 512]), in0=v4(pb[:]),
                                                                                  in1=v4(dm[:, half * 512:(half + 1) * 512]), op=ALU.mult),
                               [Rpb, Rc], [RsTm])
                        yield
                        for hh in range(4):
                            h = 4 * half + hh
                            j, p0 = h // 2, (h % 2) * 64
                            pe(lambda e, hh=hh, h=h: e.matmul(pIO[:, hh * 128: hh * 128 + 64], lhsT=sTm[:, h * 128:(h + 1) * 128],
                                                              rhs=Vb[:, h * 64:(h + 1) * 64], start=True, stop=True),
                               [RsTm, RVb], [RpIO], sig=False)
                            if not smp:
                                pe(lambda e, hh=hh, j=j, p0=p0: e.matmul(pIO[:, hh * 128 + 64:(hh + 1) * 128],
                                                                         lhsT=qkT[p0:p0 + 64, j * 128:(j + 1) * 128],
                                                                         rhs=Sb_[p0:p0 + 64, j * 64:(j + 1) * 64], start=True, stop=True),
                                   [RqkT, RSb], [RpIO], sig=(hh == 3))
                            else:
                                for r in range(2):
                                    pe(lambda e, hh=hh, j=j, p0=p0, r=r: e.matmul(pIO[:, hh * 128 + 64:(hh + 1) * 128],
                                                                                  lhsT=qab[p0:p0 + 64, r * 512 + j * 128: r * 512 + (j + 1) * 128],
                                                                                  rhs=SbS[r][p0:p0 + 64, j * 64:(j + 1) * 64],
                                                                                  start=(r == 0), stop=(r == 1)),
                                       [Rqab, RSbS[r]], [RpIO], sig=(hh == 3 and r == 1))
                        pio4 = pIO[:].rearrange("p (h t e) -> p h t e", h=4, t=2)
                        tc_, Rtc = tcr.next()
                        tc3 = tc_[:, 0:256].rearrange("p (h e) -> p h e", h=4)
                        dv(lambda e, half=half: e.tensor_tensor(out=tc3, in0=pio4[:, :, 1, :],
                                                                in1=dc[:, 4 * half:4 * half + 4].unsqueeze(2).to_broadcast([128, 4, 64]),
                                                                op=ALU.mult), [RpIO, Rc], [Rtc])
                        dv(lambda e, half=half: e.tensor_tensor(out=ro3[:, 4 * half:4 * half + 4, :], in0=tc3, in1=pio4[:, :, 0, :], op=ALU.add),
                           [Rtc, RpIO], [Rro])
                        yield
                    dv(lambda e: e.tensor_tensor(out=sqt[:], in0=ro[:], in1=ro[:], op=ALU.mult), [Rro], [Rsqt])
                    dv(lambda e: e.tensor_reduce(out=ss8[:], in_=sqt[:].rearrange("p (h e) -> p h e", h=8), axis=AX.X, op=ALU.add), [Rsqt], [Rss8])
                    ac(lambda e: e.activation(out=ss8[:], in_=ss8[:], func=AF.Sqrt, bias=epsc[:, 0:1], scale=1.0 / 64), [Rss8, Rc], [Rss8])
                    dv(lambda e: e.reciprocal(out=ss8[:], in_=ss8[:]), [Rss8], [Rss8])
                    dv(lambda e: e.tensor_tensor(out=ro3, in0=ro3, in1=ss8[:].unsqueeze(2).to_broadcast([128, 8, 64]), op=ALU.mult), [Rro, Rss8], [Rro])
                    dv(lambda e: e.tensor_tensor(out=ro[:], in0=ro[:], in1=gr[:], op=ALU.mult), [Rro, Rc], [Rro])
                    mixb, Rmixb = mixr.next()
                    dv(lambda e: e.tensor_tensor(out=mixb[:], in0=ro[:], in1=gate[:], op=ALU.mult), [Rro, Rgate], [Rmixb])
                    yield
                    transp4(mixb, Rmixb, pT2, RpT2, 0)
                    mrs, Rmrs, dmrs = mrr.next()
                    ac(lambda e: e.copy(out=mrs[:], in_=pT2[:, 0:512]), [RpT2], [Rmrs])
                    S.dma(S.pool, dmrs, MRS[:, :, qb * 128:(qb + 1) * 128].rearrange("c p t -> p c t"),
                          mrs[:].rearrange("p (c t) -> p c t", c=4), reads=[Rmrs], writes=[RMRS])
                    yield
                if not smp:
                    for j in range(4):
                        pe(lambda e, j=j: e.matmul(pSt[:, j * 128:(j + 1) * 128], lhsT=Kd[:, j * 128:(j + 1) * 128],
                                                   rhs=Vb[:, j * 128:(j + 1) * 128], start=True, stop=True),
                           [RKd, RVb], [RpSt], sig=(j == 3))
                    for par in range(2):
                        prt = slice(par * 64, par * 64 + 64)
                        Sf3 = Sf[prt, :].rearrange("p (j e) -> p j e", j=4)
                        pst3 = pSt[prt, :].rearrange("p (j x) -> p j x", j=4)[:, :, par * 64:par * 64 + 64]
                        gb = dc[prt, 16:20].unsqueeze(2).to_broadcast([64, 4, 64])
                        dv(lambda e, Sf3=Sf3, gb=gb: e.tensor_tensor(out=Sf3, in0=Sf3, in1=gb, op=ALU.mult), [RSf, Rc], [RSf])
                        dv(lambda e, Sf3=Sf3, pst3=pst3: e.tensor_tensor(out=Sf3, in0=Sf3, in1=pst3, op=ALU.add), [RSf, RpSt], [RSf])
                    ac(lambda e: e.copy(out=Sb_[:], in_=Sf[:]), [RSf], [RSb])
                    if c.get("last_prompt"):
                        for par in range(2):
                            dst = st_p.rearrange("(j par) d e -> par d j e", par=2)[par]
                            S.dma(S.pool, d_stp, dst, Sf[par * 64:(par + 1) * 64, :].rearrange("p (j e) -> p j e", j=4), reads=[RSf])
                else:
                    SfS, RSfS = SfS2[idx], RSfS2[idx]
                    for r in range(2):
                        rows = slice(r * 64, r * 64 + 64)
                        pst, Rpst = (pSt, RpSt) if r == 0 else (pSc, RpSc)
                        for j in range(4):
                            pe(lambda e, j=j, pst=pst, rows=rows: e.matmul(pst[:, j * 128:(j + 1) * 128], lhsT=Kd[rows, j * 128:(j + 1) * 128],
                                                                           rhs=Vb[rows, j * 128:(j + 1) * 128], start=True, stop=True),
                               [RKd, RVb], [Rpst], sig=(j == 3))
                        for par in range(2):
                            prt = slice(par * 64, par * 64 + 64)
                            Sf3 = SfS[r][prt, :].rearrange("p (j e) -> p j e", j=4)
                            pst3 = pst[prt, :].rearrange("p (j x) -> p j x", j=4)[:, :, par * 64:par * 64 + 64]
                            gb = dc[prt, 16:20].unsqueeze(2).to_broadcast([64, 4, 64])
                            dv(lambda e, Sf3=Sf3, gb=gb: e.tensor_tensor(out=Sf3, in0=Sf3, in1=gb, op=ALU.mult), [RSfS[r], Rc], [RSfS[r]])
                            dv(lambda e, Sf3=Sf3, pst3=pst3: e.tensor_tensor(out=Sf3, in0=Sf3, in1=pst3, op=ALU.add), [RSfS[r], Rpst], [RSfS[r]])
                        for par in range(2):
                            dst = st_s[2 * idx + r].rearrange("(j par) d e -> par d j e", par=2)[par]
                            S.dma(S.pool, d_sts2[idx][r], dst, SfS[r][par * 64:(par + 1) * 64, :].rearrange("p (j e) -> p j e", j=4), reads=[RSfS[r]])
                yield

            seq = [("oth", 0, 0)]
            for i in range(NW):
                seq += [("own", i, 2 * i + 1), ("oth", i + 1, 2 * i + 2)]
            n_prompt = len(seq)
            seq += [("smp", 0, 0), ("smp", 1, 0)]
            ctxs = [dict(kind=k_, idx=i_, ks=ks_, seq=n_) for n_, (k_, i_, ks_) in enumerate(seq)]
            ctxs[n_prompt - 1]["last_prompt"] = True

            def interleave(g1, g2, hook=None, hook_at=2):
                gens = [g for g in (g1, g2) if g is not None]
                rnd = 0
                while gens:
                    for g in list(gens):
                        try:
                            next(g)
                        except StopIteration:
                            gens.remove(g)
                    rnd += 1
                    if hook is not None and rnd == hook_at:
                        hook()
                        hook = None
                if hook is not None:
                    hook()

            P0(ctxs[0])
            P0b(ctxs[0])
            for n_, c_ in enumerate(ctxs):
                nxt = ctxs[n_ + 1] if n_ + 1 < len(ctxs) else None
                if nxt is not None:
                    P0(nxt)
                interleave(P1(c_), P2(ctxs[n_ - 2]) if n_ >= 2 else None, hook=(lambda nxt=nxt: P0b(nxt)) if nxt is not None else None)
                if n_ == 1:
                    load_wout()
            for c_ in ctxs[-2:]:
                interleave(P2(c_), None)
            S.barrier()
            ckpt("p1")

        with ExitStack() as s23:
            sb, ps, ring = mk_alloc(s23)
            MD = sb("MD", [128, 4 * NQ * 128], BF16)
            MD3 = MD[:].rearrange("p (c t) -> p c t", c=4)
            gf = sb("gf", [128, 1024])
            Rgf = Res("gf")
            xr = ring("x3", [128, 1024], F32, 3, dsem=True)
            mr3r = ring("mr3", [128, 512], BF16, 3, dsem=True)
            p3 = {}

            def p3_load(qb):
                src = x_smp[qb - NW] if qb >= NW else x_own[qb]
                xt, Rx, dx = xr.next()
                S.dma(S.sp, dx, xt[:], src, writes=[Rx])
                mr, Rmr, dmr = mr3r.next()
                S.dma(S.sp, dmr, mr[:].rearrange("p (c t) -> p c t", c=4), MRS[:, :, qb * 128:(qb + 1) * 128].rearrange("c p t -> p c t"),
                      reads=[RMRS], writes=[Rmr])
                p3[qb] = (xt, Rx, mr, Rmr)

            with ExitStack() as s2:
                sb, ps, ring = mk_alloc(s2)
                KTh = sb("KTh", [128, NKS * 128], BF16)
                Vh = sb("Vh", [128, NKS * 128], BF16)
                NCH = 2 if NKS > 16 else 1
                RKc = [Res(f"KTh_c{c}") for c in range(NCH)]
                RVc = [Res(f"Vh_c{c}") for c in range(NCH)]
                QSPL = min(1024, NQ * 128)
                HB = []
                for p_ in range(2):
                    HB.append(dict(QTh=sb(f"QTh{p_}", [128, NQ * 128], BF16), GTh=sb(f"GTh{p_}", [128, NQ * 128], BF16),
                                   KSh=sb(f"KSh{p_}", [128, 256], BF16), VSh=sb(f"VSh{p_}", [128, 256], BF16),
                                   RQc=[Res(f"QTh{p_}_a"), Res(f"QTh{p_}_b")], dqc=[S.dsem(f"hq{p_}a"), S.dsem(f"hq{p_}b")],
                                   RGTh=Res(f"GTh{p_}"), RKSh=Res(f"KSh{p_}"), RVSh=Res(f"VSh{p_}"),
                                   dq=S.dsem(f"hq{p_}"), dg=S.dsem(f"hg{p_}"), dks=S.dsem(f"hks{p_}"), dvs=S.dsem(f"hvs{p_}")))
                d_hkc = [S.dsem(f"hk{c}") for c in range(NCH)]
                d_hvc = [S.dsem(f"hv{c}") for c in range(NCH)]
                ckb = ring("ckb", [128, 1024], BF16, 4, dsem=True)
                cvb = ring("cvb", [128, 1024], BF16, 4, dsem=True)
                Er = ring("E", [128, 1024], BF16, 5)
                wr = ring("w512", [128, 512], F32, 6)
                accr = ring("acc", [128, 512], F32, 4)
                hlr = ring("hl", [128, 512], BF16, 4)
                ocr = ring("oc", [128, 512], F32, 6)
                pSTr = ring("pST", [128, 1024], F32, 2, psum=True)
                pO = [ps(f"pO{c}", [128, 512], F32) for c in range(2)]
                RpO = [Res(f"pO{c}", excl=True) for c in range(2)]
                pS0, RpS0 = ps("pS0", [128, 512], F32), Res("pS0", excl=True)
                pF, RpF = ps("pF", [128, 512], F32), Res("pF", excl=True)

                deferred = []

                def flush():
                    while deferred:
                        deferred.pop(0)[1]()

                def attn_tile(h, blocks, q0, N, qb0, hb):
                    QTh, GTh, RGTh = hb["QTh"], hb["GTh"], hb["RGTh"]
                    RQTh = hb["RQc"][0] if q0 + N <= QSPL else hb["RQc"][1]
                    assert q0 + N <= QSPL or q0 >= QSPL
                    acc, Racc = accr.next()
                    acco, Racco = accr.next()
                    po(lambda e: e.memset(acco[:, 0:N], 0.0), [], [Racco])
                    started = [False, False, False]

                    def stageA(bk, first, odd):
                        cs, ce = bk["cs"], bk["ce"]
                        n = ce - cs
                        pst, Rpst = pSTr.next()
                        subs = bk["subs"]
                        for si, sbk in enumerate(subs):
                            lo, hi = sbk["lo"], sbk["hi"]
                            pe(lambda e, sbk=sbk, lo=lo, hi=hi: e.matmul(pst[:, lo - cs:hi - cs], lhsT=sbk["kt"][0:64, :], rhs=QTh[0:64, q0 + lo:q0 + hi],
                                                                          start=True, stop=True), [sbk["RK"], RQTh], [Rpst], sig=False)
                            pe(lambda e, sbk=sbk, lo=lo, hi=hi: e.matmul(pst[:, 512 + lo - cs:512 + hi - cs], lhsT=sbk["kt"][64:128, :],
                                                                          rhs=QTh[64:128, q0 + lo:q0 + hi], start=True, stop=True),
                               [sbk["RK"], RQTh], [Rpst], sig=(si == len(subs) - 1))
                        E, RE = Er.next()
                        E3 = E[:].rearrange("p (c n) -> p c n", c=2)
                        p3 = pst[:].rearrange("p (c n) -> p c n", c=2)
                        ac(lambda e: e.activation(out=E3[:, :, 0:n], in_=p3[:, :, 0:n], func=AF.Exp, scale=0.125), [Rpst], [RE])
                        for prt, cols in bk["masks"]:
                            dv(lambda e, prt=prt, cols=cols: e.memset(E3[prt, :, cols], 0.0), [], [RE])
                        if first:
                            assert cs == 0 and ce == N
                            if bk["vmask"]:
                                dv(lambda e: e.tensor_scalar(out=acc[:, 0:n], in0=E[:, 512:512 + n], scalar1=vcol[:, 0:1], scalar2=None, op0=ALU.mult),
                                   [RE, Rc], [Racc])
                            else:
                                dv(lambda e: e.tensor_copy(acc[:, 0:n], E[:, 512:512 + n]), [RE], [Racc])
                        else:
                            a_, Ra = (acco, Racco) if odd else (acc, Racc)
                            dv(lambda e: e.tensor_tensor(out=a_[:, cs:ce], in0=a_[:, cs:ce], in1=E[:, 512:512 + n], op=ALU.add), [Ra, RE], [Ra])
                        return (bk, E, RE, first)

                    def stageB(item, last):
                        bk, E, RE, first = item
                        cs, ce = bk["cs"], bk["ce"]
                        n = ce - cs
                        subs = bk["subs"]
                        for c in range(2):
                            for si, sbk in enumerate(subs):
                                lo, hi = sbk["lo"], sbk["hi"]
                                st_ = not started[c]
                                started[c] = True
                                pe(lambda e, sbk=sbk, lo=lo, hi=hi, c=c, st_=st_: e.matmul(pO[c][:, lo:hi], lhsT=sbk["v"],
                                                                                          rhs=E[:, c * 512 + lo - cs:c * 512 + hi - cs],
                                                                                          start=st_, stop=last, skip_group_check=True),
                                   [sbk["RV"], RE], [RpO[c]], sig=(c == 1 and si == len(subs) - 1))
                            if c == 0:
                                st_ = not started[2]
                                started[2] = True
                                pe(lambda e, st_=st_: e.matmul(pS0[:, cs:ce], lhsT=(validb[:] if bk["vmask"] else onesb[:]), rhs=E[:, 0:n], start=st_,
                                                               stop=last, skip_group_check=True), [Rc, RE], [RpS0], sig=False)

                    nbk = len(blocks)
                    sched = [(int(f * nbk), fn) for f, fn in deferred]
                    del deferred[:]
                    pend = []
                    for i, bk in enumerate(blocks):
                        pend.append(stageA(bk, i == 0, i % 2 == 1))
                        if len(pend) > 2:
                            stageB(pend.pop(0), False)
                        while sched and sched[0][0] <= i:
                            sched.pop(0)[1]()
                    while pend:
                        it_ = pend.pop(0)
                        stageB(it_, len(pend) == 0)
                    while sched:
                        sched.pop(0)[1]()
                    o0, Ro0 = ocr.next()
                    o1, Ro1 = ocr.next()
                    s0, Rs0 = ocr.next()
                    dv(lambda e: e.tensor_copy(o0[:, 0:N], pO[0][:, 0:N]), [RpO[0]], [Ro0])
                    ac(lambda e: e.copy(out=s0[:, 0:N], in_=pS0[:, 0:N]), [RpS0], [Rs0])
                    dv(lambda e: e.tensor_copy(o1[:, 0:N], pO[1][:, 0:N]), [RpO[1]], [Ro1])
                    hi, Rhi = hlr.next()
                    lo, Rlo = hlr.next()
                    do, Rdo = wr.next()
                    s1, Rs1 = wr.next()
                    rs_, Rrs = wr.next()

                    def D1():
                        dv(lambda e: e.tensor_tensor(out=acc[:, 0:N], in0=acc[:, 0:N], in1=acco[:, 0:N], op=ALU.add), [Racc, Racco], [Racc])
                        po(lambda e: e.tensor_copy(hi[:, 0:N], acc[:, 0:N]), [Racc], [Rhi])

                    def D2():
                        pe(lambda e: e.matmul(pF[:, 0:N], lhsT=onesb[:], rhs=hi[:, 0:N], start=True, stop=True), [Rc, Rhi], [RpF])

                    def D2b():
                        ac(lambda e: e.copy(out=s1[:, 0:N], in_=pF[:, 0:N]), [RpF], [Rs1])

                    def mkR(t_, Rt, q):
                        def R():
                            dv(lambda e: e.reciprocal(out=t_[:, q * 128:(q + 1) * 128], in_=t_[:, q * 128:(q + 1) * 128]), [Rt], [Rt])
                        return R

                    Rs = [mkR(s0, Rs0, q) for q in range(N // 128)] + [mkR(s1, Rs1, q) for q in range(N // 128)]

                    def D3():
                        po(lambda e: e.tensor_tensor(out=o0[:, 0:N], in0=o0[:, 0:N], in1=s0[:, 0:N], op=ALU.mult), [Ro0, Rs0], [Ro0])
                        po(lambda e: e.tensor_tensor(out=o1[:, 0:N], in0=o1[:, 0:N], in1=s1[:, 0:N], op=ALU.mult), [Ro1, Rs1], [Ro1])
                        dv(lambda e: e.scalar_tensor_tensor(out=do[:, 0:N], in0=o1[:, 0:N], scalar=nlam[:, 0:1], in1=o0[:, 0:N],
                                                            op0=ALU.mult, op1=ALU.add), [Ro0, Ro1, Rc], [Rdo])
                        po(lambda e: e.tensor_tensor(out=hi[:, 0:N], in0=do[:, 0:N], in1=do[:, 0:N], op=ALU.mult), [Rdo], [Rhi])

                    def D4():
                        pe(lambda e: e.matmul(pF[:, 0:N], lhsT=onesb[:], rhs=hi[:, 0:N], start=True, stop=True), [Rc, Rhi], [RpF])

                    def D5():
                        ac(lambda e: e.activation(out=rs_[:, 0:N], in_=pF[:, 0:N], func=AF.Ln, bias=epsc[:, 0:1], scale=1.0 / 128), [RpF, Rc], [Rrs])
                        ac(lambda e: e.activation(out=rs_[:, 0:N], in_=rs_[:, 0:N], func=AF.Exp, scale=-0.5), [Rrs], [Rrs])
                        po(lambda e: e.tensor_tensor(out=do[:, 0:N], in0=do[:, 0:N], in1=rs_[:, 0:N], op=ALU.mult), [Rdo, Rrs], [Rdo])
                        nb = N // 128
                        dv(lambda e: e.scalar_tensor_tensor(out=MD3[:, h, q0:q0 + N], in0=do[:, 0:N], scalar=gd08[:, 0:1], in1=GTh[:, q0:q0 + N],
                                                            op0=ALU.mult, op1=ALU.mult), [Rdo, Rc, RGTh], [RMD[h][qb0 + i] for i in range(nb)])

                    stages = [(0.03, D1), (0.12, D2), (0.2, D2b)] + [(0.24 + 0.035 * k, R) for k, R in enumerate(Rs)] + [(0.56, D3), (0.76, D4), (0.9, D5)]
                    deferred.extend(stages)

                def head_loads(h):
                    B = HB[h % 2]
                    vsh = VS[h].rearrange("p k e -> p (k e)")
                    S.dma(S.sp, B["dqc"][0], B["QTh"][:, 0:QSPL], QT[h][:, 0:QSPL], reads=[RQT], writes=[B["RQc"][0]])
                    for c in range(NCH):
                        c0, c1 = (0, min(NKS, 16) * 128) if c == 0 else (16 * 128, NKS * 128)
                        S.dma(S.sp, d_hkc[c], KTh[:, c0:c1], KT[h][:, c0:c1], reads=[RKT], writes=[RKc[c]])
                        S.dma(S.sp, d_hvc[c], Vh[:, c0:c1], vsh[:, c0:c1], reads=[RVS], writes=[RVc[c]])
                        if c == 0 and QSPL < NQ * 128:
                            S.dma(S.sp, B["dqc"][1], B["QTh"][:, QSPL:], QT[h][:, QSPL:], reads=[RQT], writes=[B["RQc"][1]])
                    S.dma(S.sp, B["dg"], B["GTh"][:], GT[h], reads=[RGT], writes=[B["RGTh"]])
                    S.dma(S.sp, B["dks"], B["KSh"][:], KTS[h], reads=[RKT], writes=[B["RKSh"]])
                    S.dma(S.sp, B["dvs"], B["VSh"][:].rearrange("p (k e) -> p k e", e=128), VSS[:, :, h * 128:(h + 1) * 128].rearrange("k p e -> p k e"),
                          reads=[RVS], writes=[B["RVSh"]])

                head_loads(0)
                S.dma(S.sp, d_c, gf[:], gf_d[:, :], writes=[Rgf])
                p3_load(0)
                p3_load(1)
                for h in range(4):
                    B = HB[h % 2]
                    KSh, VSh, RKSh, RVSh = B["KSh"], B["VSh"], B["RKSh"], B["RVSh"]
                    for T in range(NT):
                        blocks = []
                        for ks in range(8 * T + 8):
                            m = ks - 8 * T
                            cs = 0 if m <= 0 else 128 * (m // 2)
                            masks = []
                            if m >= 1 and m % 2 == 1:
                                masks.append((slice(64, 128), slice(0, 64)))
                            blocks.append(dict(subs=[dict(kt=KTh[:, ks * 128:(ks + 1) * 128], RK=RKc[min(ks // 16, NCH - 1)], v=Vh[:, ks * 128:(ks + 1) * 128], RV=RVc[min(ks // 16, NCH - 1)],
                                                          lo=cs, hi=512)], vmask=(ks == 0), cs=cs, ce=512, masks=masks))
                        attn_tile(h, blocks, T * 512, 512, 4 * T, B)
                        if T == min(1, NT - 1):
                            cks, cvs = [], []
                            for sidx in range(4):
                                kb_, Rkb, dkb_ = ckb.next()
                                vb_, Rvb, dvb_ = cvb.next()
                                S.dma(S.pool, dkb_, kb_[:], ckT_d[sidx, h], writes=[Rkb])
                                S.dma(S.pool, dvb_, vb_[:].rearrange("p (k e) -> p k e", e=128),
                                      cv_d[sidx].rearrange("(k p) e -> p k e", p=128)[:, :, h * 128:(h + 1) * 128], writes=[Rvb])
                                cks.append((kb_, Rkb))
                                cvs.append((vb_, Rvb))
                    if h + 1 < 4:
                        head_loads(h + 1)
                    blocks = [dict(subs=[dict(kt=KSh[:, b * 128:(b + 1) * 128], RK=RKSh, v=VSh[:, b * 128:(b + 1) * 128], RV=RVSh,
                                              lo=b * 128, hi=(b + 1) * 128) for b in range(2)],
                                   vmask=False, cs=0, ce=256,
                                   masks=[(slice(0, 64), slice(64, 128)), (slice(64, 128), slice(0, 64)),
                                          (slice(0, 64), slice(192, 256)), (slice(64, 128), slice(128, 192))])]
                    for kk in range(8):
                        blocks.append(dict(subs=[dict(kt=cks[s_][0][:, kk * 128:(kk + 1) * 128], RK=cks[s_][1],
                                                      v=cvs[s_][0][:, kk * 128:(kk + 1) * 128], RV=cvs[s_][1], lo=s_ * 64, hi=s_ * 64 + 64)
                                                 for s_ in range(4)], vmask=False, cs=0, ce=256, masks=[]))
                    attn_tile(h, blocks, NW * 128, 256, NW, B)
                flush()
                S.barrier()
                ckpt("p2")

            with ExitStack() as s3:
                sb, ps, ring = mk_alloc(s3)
                xsr = ring("xs3", [128, 1024], F32, 2)
                yor = ring("yo3", [128, 1024], F32, 2, dsem=True)
                junk3 = sb("junk3", [128, 1024], BF16)
                Rj3 = Res("junk3")
                ss3 = ring("ss3", [128, 2], F32, 2)
                pyr = ring("py", [128, 512], F32, 4, psum=True)
                for qb in range(NQ):
                    smp = qb >= NW
                    dst = y_smp[qb - NW] if smp else y_own[qb]
                    if qb + 2 < NQ:
                        p3_load(qb + 2)
                    xt, Rx, mr, Rmr = p3[qb]
                    xs, Rxs = xsr.next()
                    for n in range(2):
                        py, Rpy = pyr.next()
                        for c in range(8):
                            lhsT = mr[:, c * 128:(c + 1) * 128] if c < 4 else MD3[:, c - 4, qb * 128:(qb + 1) * 128]
                            Rl = Rmr if c < 4 else RMD[c - 4][qb]
                            pe(lambda e, lhsT=lhsT, c=c, n=n, py=py: e.matmul(py[:], lhsT=lhsT, rhs=woutb[:, c * 1024 + n * 512: c * 1024 + (n + 1) * 512],
                                                                              start=(c == 0), stop=(c == 7)), [Rl, Rwout], [Rpy], sig=(c == 7))
                        dv(lambda e, n=n, py=py: e.tensor_tensor(out=xs[:, n * 512:(n + 1) * 512], in0=py[:], in1=xt[:, n * 512:(n + 1) * 512], op=ALU.add),
                           [Rpy, Rx], [Rxs])
                    ss, Rss = ss3.next()
                    ac(lambda e: e.activation(out=junk3[:], in_=xs[:], func=AF.Square, accum_out=ss[:, 0:1]), [Rxs], [Rj3, Rss])
                    ac(lambda e: e.activation(out=ss[:, 1:2], in_=ss[:, 0:1], func=AF.Sqrt, bias=epsc[:, 0:1], scale=1.0 / 1024), [Rss, Rc], [Rss])
                    dv(lambda e: e.reciprocal(out=ss[:, 1:2], in_=ss[:, 1:2]), [Rss], [Rss])
                    yo, Ryo, dyo = yor.next()
                    dv(lambda e: e.scalar_tensor_tensor(out=yo[:], in0=xs[:], scalar=ss[:, 1:2], in1=gf[:], op0=ALU.mult, op1=ALU.mult), [Rxs, Rss, Rgf], [Ryo])
                    S.dma(S.pool, dyo, dst, yo[:], reads=[Ryo])
                S.barrier()
    return nc


def _tables(pos):
    pos = np.asarray(pos, np.float32)
    t = np.zeros((128, TABW), np.float32)
    inv_r = (1.0 / (np.float32(10000.0) ** (np.arange(0, 64, 2, dtype=np.float32) / np.float32(64)))).astype(np.float32)
    ang = (pos[:, None] * inv_r[None, :]).astype(np.float32)
    c, s = np.cos(ang), np.sin(ang)
    t[:, 0:32] = c
    t[:, 32:64] = c
    t[:, 64:96] = -s
    t[:, 96:128] = s
    inv_d = (1.0 / (np.float32(500000.0) ** (np.arange(0, 16, 2, dtype=np.float32) / np.float32(16)))).astype(np.float32)
    ang = (pos[:, None] * inv_d[None, :]).astype(np.float32)
    c, s = np.cos(ang), np.sin(ang)
    t[:, 128:136] = c
    t[:, 136:144] = c
    t[:, 144:192] = 1.0
    t[:, 192:200] = -s
    t[:, 200:208] = s
    return t


def _decay_consts(L):
    lg = np.log1p(-np.exp2(-5.0 - np.arange(8, dtype=np.float64)))
    p = np.arange(128)
    loc = p % L
    blk = p // L
    rel = loc[None, :] - loc[:, None]
    ok = (rel >= 0) & (blk[None, :] == blk[:, None])
    dm = np.zeros((128, 8, 128), np.float64)
    for h in range(8):
        dm[:, h, :] = np.where(ok, np.exp(lg[h] * np.maximum(rel, 0)), 0.0)
    dect = np.zeros((128, 20), np.float64)
    dect[:, 0:8] = np.exp(lg[None, :] * (loc[:, None] + 1.0))
    dect[:, 8:16] = np.exp(lg[None, :] * (L - 1.0 - loc[:, None]))
    for j in range(4):
        dect[0:64, 16 + j] = np.exp(lg[2 * j] * L)
        dect[64:128, 16 + j] = np.exp(lg[2 * j + 1] * L)
    return dm.reshape(128, 1024).astype(np.float32), dect.astype(np.float32)


_NC_CACHE = {}
_STOP = None


def _run(inputs, NT):
    f = lambda a: np.ascontiguousarray(np.asarray(a, dtype=np.float32))
    xp = f(inputs["x_prompt"])
    xs = f(inputs["x_sample"])
    ck = f(inputs["cache_k"])[0]
    cv = f(inputs["cache_v"])[0]
    sr = f(inputs["state_ret"])[0]
    B, L = xp.shape[0], xp.shape[1]
    assert L == 1024 * NT and B == 4
    NW, NO = 4 * NT, 4 * NT + 1
    past = ck.shape[1]
    dm_p, dc_p = _decay_consts(128)
    dm_s, dc_s = _decay_consts(64)
    rep = lambda v, n=128: np.ascontiguousarray(np.broadcast_to(np.asarray(v, np.float32).reshape(1, -1), (n, np.asarray(v).size)))
    lamv = np.concatenate([f(inputs[k])[0] for k in ("lam_q1", "lam_k1", "lam_q2", "lam_k2")])
    common = {
        "w_in": f(inputs["w_in"])[0], "w_out": f(inputs["w_out"])[0],
        "gn": rep(f(inputs["norm_g"])[0]), "gf": rep(f(inputs["final_norm_g"])), "gr": rep(f(inputs["ret_norm_g"])[0]),
        "gd": f(inputs["diff_norm_g"])[0].reshape(128, 1).copy(), "lamv": rep(lamv),
        "ident": np.eye(128, dtype=np.float32), "dm_p": dm_p, "dm_s": dm_s, "dect_p": dc_p, "dect_s": dc_s,
    }
    t_smp_blk = _tables(past + (np.arange(128) % 64))
    in_maps = []
    for c in range(8):
        b, s = c // 2, c % 2
        xb = xp[b].reshape(2 * NW, 128, 1024)
        own_blocks = [2 * i + s for i in range(NW)]
        oth_blocks = [(2 * i if s == 1 else 2 * i - 1) for i in range(NO)]
        x_own = xb[own_blocks]
        x_oth = np.zeros((NO, 128, 1024), np.float32)
        t_oth = np.zeros((NO, 128, TABW), np.float32)
        for i, blk in enumerate(oth_blocks):
            if 0 <= blk < 2 * NW:
                x_oth[i] = xb[blk]
                t_oth[i] = _tables(blk * 128 + np.arange(128))
            else:
                t_oth[i] = _tables(np.zeros(128))
        t_own = np.stack([_tables(blk * 128 + np.arange(128)) for blk in own_blocks])
        streams = [4 * c + r for r in range(4)]
        m = dict(common)
        m.update({
            "x_oth": x_oth, "x_own": np.ascontiguousarray(x_own), "x_smp": np.ascontiguousarray(xs[streams].reshape(2, 128, 1024)),
            "t_oth": t_oth, "t_own": t_own, "t_smp": np.stack([t_smp_blk, t_smp_blk]),
            "valid0": (np.ones((128, 128), np.float32) if s == 1 else np.zeros((128, 128), np.float32)),
            "cache_kT": np.ascontiguousarray(ck[streams].reshape(4, past, 4, 128).transpose(0, 2, 3, 1)),
            "cache_v": np.ascontiguousarray(cv[streams].reshape(4, past, 512)),
            "state_in": np.ascontiguousarray(sr[streams]),
        })
        in_maps.append(m)
    if NT not in _NC_CACHE:
        _NC_CACHE[NT] = build(NT, _STOP)
    nc = _NC_CACHE[NT]
    res = run_bass_kernel_spmd(nc, in_maps, core_ids=list(range(8)))
    R = res.results
    y_p = np.zeros((4, 2 * NW, 128, 1024), np.float32)
    k_p = np.zeros((4, 2 * NW, 128, 512), np.float32)
    v_p = np.zeros((4, 2 * NW, 128, 512), np.float32)
    st_p = np.zeros((4, 8, 64, 64), np.float32)
    y_s = np.zeros((32, 64, 1024), np.float32)
    k_s = np.zeros((32, 64, 512), np.float32)
    v_s = np.zeros((32, 64, 512), np.float32)
    st_s = np.zeros((32, 8, 64, 64), np.float32)
    for c in range(8):
        b, s = c // 2, c % 2
        r = R[c]
        y_p[b, s::2] = r["y_own"]
        k_p[b, s::2] = r["k_own"]
        v_p[b, s::2] = r["v_own"]
        if s == 0:
            st_p[b] = r["st_p"]
        y_s[4 * c:4 * c + 4] = r["y_smp"].reshape(4, 64, 1024)
        k_s[4 * c:4 * c + 4] = r["k_smp"].reshape(4, 64, 512)
        v_s[4 * c:4 * c + 4] = r["v_smp"].reshape(4, 64, 512)
        st_s[4 * c:4 * c + 4] = r["st_s"]
    return (y_p.reshape(4, L, 1024), y_s, st_p[None], st_s[None], k_p.reshape(1, 4, L, 4, 128), v_p.reshape(1, 4, L, 4, 128),
            k_s.reshape(1, 32, 64, 4, 128), v_s.reshape(1, 32, 64, 4, 128))


def kernel(**inputs):
    return _run(inputs, 8)
```
